# Optimizing a Trainium2 kernel written in Bass

```python
import jax, jax.numpy as jnp
from jax import lax
import numpy as np

D_MODEL = 1024
BATCH = 4
SEQ = 8192
DEPTH = 1

N_META = 16
ATT_HEAD_DIM = 128
ATT_HEADS = D_MODEL // ATT_HEAD_DIM
ATT_WIDTH = ATT_HEADS * ATT_HEAD_DIM
LRU_WIDTH = D_MODEL
LRU_BLOCK_DIM = 64
LRU_BLOCKS = LRU_WIDTH // LRU_BLOCK_DIM
MIX_WIDTH = ATT_WIDTH + LRU_WIDTH
IN_WIDTH = 4 * ATT_WIDTH + 2 * LRU_WIDTH
CONV_WIDTH = 4
LRU_C = 8.0
Q_BLOCK = 128
RMS_EPS = 1e-6

kernel_name = "hymba_stickbreak_rglru_hybrid"


def rmsnorm(x, g):
    xf = x.astype(jnp.float32)
    y = xf * lax.rsqrt(jnp.mean(xf * xf, axis=-1, keepdims=True) + RMS_EPS)
    return (y * g.astype(jnp.float32)).astype(x.dtype)


def stick_breaking_attention(q, k, v):
    B, T, _ = q.shape
    pad = Q_BLOCK - N_META
    Tp = T + pad
    nb = Tp // Q_BLOCK
    scale = 1.0 / np.sqrt(ATT_HEAD_DIM).astype(np.float32)

    def to_heads(t):
        t = jnp.pad(t.astype(jnp.float32), ((0, 0), (pad, 0), (0, 0)))
        return t.reshape(B, Tp, ATT_HEADS, ATT_HEAD_DIM).transpose(0, 2, 1, 3)

    qh, kh, vh = to_heads(q), to_heads(k), to_heads(v)
    q_blocks = qh.reshape(B, ATT_HEADS, nb, Q_BLOCK, ATT_HEAD_DIM).transpose(2, 0, 1, 3, 4)
    kpos = jnp.arange(Tp)

    def one_block(args):
        i, qi = args
        qpos = i * Q_BLOCK + jnp.arange(Q_BLOCK)
        mask = (kpos[None, :] < qpos[:, None]) & (kpos[None, :] >= pad)
        z = jnp.einsum('bhqd,bhkd->bhqk', qi, kh) * scale
        log_beta = jax.nn.log_sigmoid(z)
        log_1m = jnp.where(mask, log_beta - z, 0.0)
        suffix = lax.cumsum(log_1m, axis=3, reverse=True) - log_1m
        w = jnp.where(mask, jnp.exp(log_beta + suffix), 0.0)
        return jnp.einsum('bhqk,bhkd->bhqd', w, vh)

    o = lax.map(one_block, (jnp.arange(nb), q_blocks))
    o = o.transpose(1, 0, 3, 2, 4).reshape(B, Tp, ATT_WIDTH)[:, pad:]
    return o.astype(q.dtype)


def rglru_branch(x, conv_w, conv_b, gate_a_w, gate_a_b, gate_x_w, gate_x_b, lru_lambda):
    B, T, _ = x.shape
    xp = jnp.pad(x, ((0, 0), (CONV_WIDTH - 1, 0), (0, 0)))
    xc = conv_b + sum(xp[:, j:j + T] * conv_w[j] for j in range(CONV_WIDTH))
    xb = xc.reshape(B, T, LRU_BLOCKS, LRU_BLOCK_DIM)
    r = jax.nn.sigmoid(jnp.einsum('btni,nij->btnj', xb, gate_a_w).reshape(B, T, LRU_WIDTH) + gate_a_b)
    i = jax.nn.sigmoid(jnp.einsum('btni,nij->btnj', xb, gate_x_w).reshape(B, T, LRU_WIDTH) + gate_x_b)
    log_a = (LRU_C * r.astype(jnp.float32)) * jax.nn.log_sigmoid(lru_lambda.astype(jnp.float32))
    a = jnp.exp(log_a)
    b = jnp.sqrt(-jnp.expm1(2.0 * log_a)) * (i * xc).astype(jnp.float32)

    def combine(left, right):
        a1, b1 = left
        a2, b2 = right
        return a1 * a2, a2 * b1 + b2

    _, h = lax.associative_scan(combine, (a, b), axis=1)
    return h.astype(x.dtype)


def hybrid_layer(h, pre_g, post_g, w_in, w_out, att_out_g, lru_out_g, conv_w, conv_b,
                 gate_a_w, gate_a_b, gate_x_w, gate_x_b, lru_lambda):
    u = rmsnorm(h, pre_g)
    p = u @ w_in
    A = ATT_WIDTH
    q, k, v, g_att, x_lru, g_lru = jnp.split(p, [A, 2 * A, 3 * A, 4 * A, 4 * A + LRU_WIDTH], axis=-1)
    att = rmsnorm(stick_breaking_attention(q, k, v), att_out_g) * jax.nn.silu(g_att)
    lru = rglru_branch(x_lru, conv_w, conv_b, gate_a_w, gate_a_b, gate_x_w, gate_x_b, lru_lambda)
    lru = rmsnorm(lru, lru_out_g) * jax.nn.silu(g_lru)
    y = jnp.concatenate([att, lru], axis=-1) @ w_out
    return h + rmsnorm(y, post_g)


def setup_inputs(seed: int = 0) -> dict:
    key = jax.random.key(seed)
    ks = jax.random.split(key, 16)
    f32 = jnp.float32
    x = jax.random.normal(ks[0], (BATCH, SEQ, D_MODEL), f32)
    meta_tokens = jax.random.normal(ks[1], (N_META, D_MODEL), f32)
    pre_g = 1.0 + 0.02 * jax.random.normal(ks[2], (DEPTH, D_MODEL), f32)
    post_g = 1.0 + 0.02 * jax.random.normal(ks[3], (DEPTH, D_MODEL), f32)
    w_in = jax.random.normal(ks[4], (DEPTH, D_MODEL, IN_WIDTH), f32) * D_MODEL ** -0.5
    w_out = jax.random.normal(ks[5], (DEPTH, MIX_WIDTH, D_MODEL), f32) * MIX_WIDTH ** -0.5
    att_out_g = 1.0 + 0.02 * jax.random.normal(ks[6], (DEPTH, ATT_WIDTH), f32)
    lru_out_g = 1.0 + 0.02 * jax.random.normal(ks[7], (DEPTH, LRU_WIDTH), f32)
    conv_w = jax.random.normal(ks[8], (DEPTH, CONV_WIDTH, LRU_WIDTH), f32) * CONV_WIDTH ** -0.5
    conv_b = 0.1 * jax.random.normal(ks[9], (DEPTH, LRU_WIDTH), f32)
    gate_a_w = jax.random.normal(ks[10], (DEPTH, LRU_BLOCKS, LRU_BLOCK_DIM, LRU_BLOCK_DIM), f32) * LRU_BLOCK_DIM ** -0.5
    gate_a_b = 0.1 * jax.random.normal(ks[11], (DEPTH, LRU_WIDTH), f32)
    gate_x_w = jax.random.normal(ks[12], (DEPTH, LRU_BLOCKS, LRU_BLOCK_DIM, LRU_BLOCK_DIM), f32) * LRU_BLOCK_DIM ** -0.5
    gate_x_b = 0.1 * jax.random.normal(ks[13], (DEPTH, LRU_WIDTH), f32)
    a_c = jax.random.uniform(ks[14], (DEPTH, LRU_WIDTH), f32, 0.9, 0.999)
    a0 = a_c ** (1.0 / LRU_C)
    lru_lambda = jnp.log(a0) - jnp.log1p(-a0)
    return {"x": x, "meta_tokens": meta_tokens, "pre_g": pre_g, "post_g": post_g,
            "w_in": w_in, "w_out": w_out, "att_out_g": att_out_g, "lru_out_g": lru_out_g,
            "conv_w": conv_w, "conv_b": conv_b, "gate_a_w": gate_a_w, "gate_a_b": gate_a_b,
            "gate_x_w": gate_x_w, "gate_x_b": gate_x_b, "lru_lambda": lru_lambda}


def reference(x, meta_tokens, pre_g, post_g, w_in, w_out, att_out_g, lru_out_g, conv_w, conv_b,
              gate_a_w, gate_a_b, gate_x_w, gate_x_b, lru_lambda):
    B = x.shape[0]
    meta = jnp.broadcast_to(meta_tokens.astype(x.dtype)[None], (B, N_META, D_MODEL))
    h = jnp.concatenate([meta, x], axis=1)
    for l in range(DEPTH):
        h = hybrid_layer(h, pre_g[l], post_g[l], w_in[l], w_out[l], att_out_g[l], lru_out_g[l],
                         conv_w[l], conv_b[l], gate_a_w[l], gate_a_b[l], gate_x_w[l], gate_x_b[l],
                         lru_lambda[l])
    return h[:, N_META:]
```

```python
import numpy as np
import ml_dtypes
from contextlib import ExitStack
import concourse.bass as bass
import concourse.mybir as mybir
from concourse.bass_utils import run_bass_kernel_spmd

F32 = mybir.dt.float32
BF16 = mybir.dt.bfloat16
AF = mybir.ActivationFunctionType
ALU = mybir.AluOpType

D = 1024
NCH = 8
H = 8
N_META = 16
EPS = 1e-6
NV = 88


class Sem:
    __slots__ = ("h", "val", "eng")

    def __init__(self, h, eng=None):
        self.h = h
        self.val = 0
        self.eng = eng


class Buf:
    __slots__ = ("name", "w", "r", "excl")

    def __init__(self, name="", excl=False):
        self.name = name
        self.w = {}
        self.r = {}
        self.excl = excl


def PBuf():
    return Buf("psum", True)


class Sched:
    EPOCH = 30000

    def __init__(self, nc):
        self.nc = nc
        self.eng = dict(pe=nc.tensor, act=nc.scalar, dve=nc.vector, pool=nc.gpsimd, sp=nc.sync)
        self.cur = {}
        self.know = {e: {} for e in self.eng}
        self.clock = {}
        self.sems = []
        self.ninst = {e: 0 for e in self.eng}
        self.nwait = {e: 0 for e in self.eng}

    def new_sem(self, name, eng=None):
        s = Sem(self.nc.alloc_semaphore(name=f"{name}_{len(self.sems)}"), eng)
        self.sems.append(s)
        return s

    def _next_tok(self, e):
        s = self.cur.get(e)
        if s is None or s.val >= self.EPOCH:
            s = self.cur[e] = self.new_sem("e" + e, e)
        s.val += 1
        return s, s.val

    def _wait(self, e, sem, val):
        kn = self.know[e]
        if kn.get(sem, 0) >= val:
            return
        self.eng[e].wait_ge(sem.h, val)
        self.nwait[e] += 1
        for s2, v2 in self.clock[(sem, val)].items():
            if kn.get(s2, 0) < v2:
                kn[s2] = v2

    def _deps(self, e, reads, writes):
        for b in reads:
            for sem, val in b.w.items():
                if sem.eng == e and e == "pe":
                    continue
                self._wait(e, sem, val)
            if b.excl:
                for sem, val in b.r.items():
                    if sem.eng == e:
                        continue
                    self._wait(e, sem, val)
        for b in writes:
            for sem, val in b.w.items():
                if sem.eng == e and e == "pe":
                    continue
                self._wait(e, sem, val)
            for sem, val in b.r.items():
                if sem.eng == e and e == "pe":
                    continue
                self._wait(e, sem, val)

    def _record(self, e, sem, val, reads, writes):
        ck = dict(self.know[e])
        ck[sem] = val
        self.clock[(sem, val)] = ck
        for b in reads:
            if b.r.get(sem, 0) < val:
                b.r[sem] = val
        for b in writes:
            b.w = {sem: val}
            b.r = {}

    def op(self, e, inst_fn, reads=(), writes=()):
        self._deps(e, reads, writes)
        inst = inst_fn()
        sem, val = self._next_tok(e)
        inst.then_inc(sem.h, 1)
        self.ninst[e] += 1
        self._record(e, sem, val, reads, writes)

    def dma(self, q, out, in_, reads=(), writes=(), dsem=None):
        self._deps(q, reads, writes)
        inst = self.eng[q].dma_start(out=out, in_=in_)
        dsem.val += 16
        inst.then_inc(dsem.h, 16)
        self.ninst[q] += 1
        self._record(q, dsem, dsem.val, reads, writes)

    def barrier(self):
        for e in self.eng:
            for s in self.sems:
                if s.val > 0 and (s, s.val) in self.clock:
                    self._wait(e, s, s.val)


class K:
    def __init__(self, nc, S):
        self.nc = nc
        self.S = S

    def act(self, out, in_, func, r=(), w=(), **kw):
        self.S.op("act", lambda: self.nc.scalar.activation(out=out, in_=in_, func=func, **kw), r, w)

    def mm(self, out, lhsT, rhs, start, stop, r=(), w=()):
        self.S.op("pe", lambda: self.nc.tensor.matmul(out, lhsT=lhsT, rhs=rhs, start=start, stop=stop,
                                                      skip_group_check=True), r, w)

    def tr(self, out, in_, ident, r=(), w=()):
        self.S.op("pe", lambda: self.nc.tensor.transpose(out=out, in_=in_, identity=ident), r, w)

    def _v(self, e):
        return self.nc.vector if e == "dve" else self.nc.gpsimd

    def tt(self, e, out, in0, in1, op, r=(), w=()):
        self.S.op(e, lambda: self._v(e).tensor_tensor(out=out, in0=in0, in1=in1, op=op), r, w)

    def ts(self, e, out, in0, s1, s2, op0, op1=None, r=(), w=()):
        if op1 is None:
            self.S.op(e, lambda: self._v(e).tensor_scalar(out=out, in0=in0, scalar1=s1, scalar2=None, op0=op0), r, w)
        else:
            self.S.op(e, lambda: self._v(e).tensor_scalar(out=out, in0=in0, scalar1=s1, scalar2=s2, op0=op0, op1=op1), r, w)

    def stt(self, e, out, in0, scalar, in1, op0, op1, r=(), w=()):
        self.S.op(e, lambda: self._v(e).scalar_tensor_tensor(out=out, in0=in0, scalar=scalar, in1=in1,
                                                             op0=op0, op1=op1), r, w)

    def cp(self, e, out, in_, r=(), w=()):
        self.S.op(e, lambda: self._v(e).tensor_copy(out=out, in_=in_), r, w)

    def memset(self, e, ap, val, r=(), w=()):
        self.S.op(e, lambda: self._v(e).memset(ap, val), r, w)

    def scan(self, out, d0, d1, initial, r=(), w=()):
        self.S.op("dve", lambda: self.nc.vector.tensor_tensor_scan(out=out, data0=d0, data1=d1, initial=initial,
                                                                   op0=ALU.mult, op1=ALU.add), r, w)


def build(SEQ):
    NQ = SEQ // 256
    assert NQ % 4 == 0
    NG = NQ // 4
    NBK = 2 * NQ + 1
    TOK = NBK * 128
    MYT = NQ * 128
    tiles = [(b0, min(4, NBK - b0)) for b0 in range(0, NBK, 4)]

    nc = bass.Bass("TRN2", target_bir_lowering=False)
    S = Sched(nc)
    k = K(nc, S)
    _uid = [0]

    def sbt(name, shape, dt):
        _uid[0] += 1
        return nc.sbuf_tensor(f"{name}_{_uid[0]}", shape, dt)

    def pst(name, shape, dt):
        _uid[0] += 1
        return nc.psum_tensor(f"{name}_{_uid[0]}", shape, dt)

    xs_d = nc.dram_tensor("xs", [TOK, D], F32, kind="ExternalInput").ap()
    win_d = nc.dram_tensor("w_in", [D, 6 * D], F32, kind="ExternalInput").ap()
    wout_d = nc.dram_tensor("w_out", [2 * D, D], F32, kind="ExternalInput").ap()
    vecs_d = nc.dram_tensor("vecs", [128, NV], F32, kind="ExternalInput").ap()
    pg_d = nc.dram_tensor("pg_bc", [128, D], F32, kind="ExternalInput").ap()
    gw_d = nc.dram_tensor("gatew", [128, 2, NCH, 128], F32, kind="ExternalInput").ap()
    cbf_d = nc.dram_tensor("cbf", [128, 4, 128], BF16, kind="ExternalInput").ap()
    cf_d = nc.dram_tensor("cf32", [128, 128], F32, kind="ExternalInput").ap()
    km_d = nc.dram_tensor("kmask", [128, 2], F32, kind="ExternalInput").ap()
    tm_d = nc.dram_tensor("tmask", [128, 256], F32, kind="ExternalInput").ap()
    y_d = nc.dram_tensor("y", [MYT, D], F32, kind="ExternalOutput").ap()

    KTs = nc.dram_tensor("KTs", [H, 128, TOK], BF16).ap()
    Vs = nc.dram_tensor("Vs", [H, 128, NBK, 128], BF16).ap()
    Qs = nc.dram_tensor("Qs", [H, 128, MYT], BF16).ap()
    Gs = nc.dram_tensor("Gs", [H, 128, MYT], BF16).ap()
    Ls = nc.dram_tensor("Ls", [NCH, 128, MYT], BF16).ap()
    bKTs, bVs, bQs, bGs, bLs = (Buf(n) for n in ("KTs", "Vs", "Qs", "Gs", "Ls"))

    win_v = win_d.rearrange("(c p) n -> p c n", p=128)
    wout_v = wout_d.rearrange("(c p) n -> p c n", p=128)

    vecs = nc.alloc_sbuf_tensor("vecs_s", [128, NV], F32)
    der = nc.alloc_sbuf_tensor("der_s", [128, 48], F32)
    pg = nc.alloc_sbuf_tensor("pg_s", [128, D], F32)
    cbf = nc.alloc_sbuf_tensor("cbf_s", [128, 4, 128], BF16)
    cf = nc.alloc_sbuf_tensor("cf_s", [128, 128], F32)
    km = nc.alloc_sbuf_tensor("km_s", [128, 2], F32)
    tm = nc.alloc_sbuf_tensor("tm_s", [128, 256], F32)
    cst = nc.alloc_sbuf_tensor("cst_s", [128, 8], F32)
    onesb = nc.alloc_sbuf_tensor("onesb_s", [128, 2], BF16)
    rl = nc.alloc_sbuf_tensor("rl_s", [128, NQ], F32)
    ssacc = nc.alloc_sbuf_tensor("ssacc_s", [128, NQ], F32)
    ra = nc.alloc_sbuf_tensor("ra_s", [128, NQ], F32)
    tmp8 = nc.alloc_sbuf_tensor("tmp8_s", [128, 64], F32)
    bvecs, bder, bpg, bcbf, bcf, bkm, btm, bcst, bonesb, brl, bssacc, bra, btmp8 = (Buf() for _ in range(13))

    trineg = cbf[:, 0, :]
    onesneg = cbf[:, 1, :]
    mask01b = cbf[:, 2, :]
    identb = cbf[:, 3, :]
    pre_g = lambda c: vecs[:, c:c + 1]
    att_g = lambda h: vecs[:, 8 + h:9 + h]
    conv_b = lambda c: vecs[:, 24 + c:25 + c]
    conv_w = lambda c, j: vecs[:, 56 + c * 4 + j:57 + c * 4 + j]
    s1 = lambda c: der[:, c:c + 1]
    s2 = lambda c: der[:, 8 + c:9 + c]
    hba = lambda c: der[:, 16 + c:17 + c]
    hbx = lambda c: der[:, 24 + c:25 + c]
    lgh = lambda c: der[:, 32 + c:33 + c]
    agh = lambda h: der[:, 40 + h:41 + h]
    one_f = cst[:, 0:1]
    nhalf = cst[:, 1:2]

    ld0 = S.new_sem("ld0")
    S.dma("sp", vecs[:], vecs_d[:, :], writes=[bvecs], dsem=ld0)
    S.dma("sp", pg[:], pg_d[:, :], writes=[bpg], dsem=ld0)
    S.dma("sp", cbf[:], cbf_d[:, :, :], writes=[bcbf], dsem=ld0)
    S.dma("sp", cf[:], cf_d[:, :], writes=[bcf], dsem=ld0)
    S.dma("sp", km[:], km_d[:, :], writes=[bkm], dsem=ld0)
    S.dma("sp", tm[:], tm_d[:, :], writes=[btm], dsem=ld0)
    for b_ in (bvecs, bpg, bcbf, bcf, bkm, btm):
        b_.w = {ld0: ld0.val}
    k.memset("pool", cst[:, 0:1], 1.0, w=[bcst])
    k.memset("pool", cst[:, 1:2], -0.5, w=[bcst])
    k.memset("pool", onesb[:], 1.0, w=[bonesb])
    k.act(tmp8[:, 0:8], vecs[:, 48:56], AF.Exp, r=[bvecs], w=[btmp8], scale=-1.0)
    k.act(tmp8[:, 8:16], tmp8[:, 0:8], AF.Ln, r=[btmp8], w=[btmp8], bias=1.0)
    k.ts("dve", der[:, 0:8], tmp8[:, 8:16], -4.0, None, ALU.mult, r=[btmp8], w=[bder])
    k.ts("dve", der[:, 8:16], tmp8[:, 8:16], 4.0, None, ALU.mult, r=[btmp8], w=[bder])
    k.ts("dve", der[:, 16:32], vecs[:, 32:48], 0.5, None, ALU.mult, r=[bvecs], w=[bder])
    k.ts("dve", der[:, 32:40], vecs[:, 16:24], 0.5, None, ALU.mult, r=[bvecs], w=[bder])
    k.ts("dve", der[:, 40:48], vecs[:, 8:16], 0.5, None, ALU.mult, r=[bvecs], w=[bder])

    def my_blocks(b0, nb):
        return [bi for bi in range(nb) if (b0 + bi) % 2 == 0 and (b0 + bi) >= 2]

    def convert_weights(nchunk, Wb, bW, src_v, col0, ncols, scale_fn, stage, bstage, dsems, order=None,
                        as_closures=False):
        ngrp = ncols // 512
        dmas, convs = [], []
        for gi_, g in enumerate(order if order is not None else range(ngrp)):
            dmas.append(lambda gi_=gi_, g=g: _cw_dma(nchunk, src_v, col0, stage, bstage, dsems, gi_, g))
            convs.append(lambda gi_=gi_, g=g: _cw_conv(nchunk, Wb, bW, scale_fn, stage, bstage, gi_, g))
        if as_closures:
            return dmas, convs
        for d_, c_ in zip(dmas, convs):
            d_()
            c_()

    def _cw_dma(nchunk, src_v, col0, stage, bstage, dsems, gi_, g):
        sl = gi_ % 2
        S.dma("sp", stage[sl][:, :nchunk, :], src_v[:, :, col0 + g * 512: col0 + (g + 1) * 512],
              writes=[bstage[sl]], dsem=dsems[sl])

    def _cw_conv(nchunk, Wb, bW, scale_fn, stage, bstage, gi_, g):
        if True:
            sl = gi_ % 2
            for c in range(nchunk):
                o_ap = Wb[:, c, g * 512:(g + 1) * 512]
                if gi_ % 2 == 0:
                    if scale_fn is None:
                        k.cp("dve", o_ap, stage[sl][:, c, :], r=[bstage[sl]], w=[bW[g]])
                    else:
                        k.ts("dve", o_ap, stage[sl][:, c, :], scale_fn(c), None, ALU.mult,
                             r=[bstage[sl], bvecs], w=[bW[g]])
                else:
                    if scale_fn is None:
                        k.act(o_ap, stage[sl][:, c, :], AF.Identity, r=[bstage[sl]], w=[bW[g]])
                    else:
                        k.act(o_ap, stage[sl][:, c, :], AF.Identity, r=[bstage[sl], bvecs], w=[bW[g]],
                              scale=scale_fn(c))

    class UT:
        def __init__(self, es, nx=6, ntp=2, scale_eng="dve"):
            self.scale_eng = scale_eng
            self.xt = [es.enter_context(sbt(f"xt{i}", [128, D], F32)) for i in range(nx)]
            self.bxt = [Buf() for _ in range(nx)]
            self.xsem = [S.new_sem("x") for _ in range(nx)]
            self.junk = es.enter_context(sbt("junk", [128, D], BF16))
            self.bjunk = Buf()
            self.ss = [es.enter_context(sbt(f"ss{i}", [128, 12], F32)) for i in range(2)]
            self.bss = [[Buf() for _ in range(4)] for _ in range(2)]
            self.xslots = {}
            self.ub = [es.enter_context(sbt(f"ub{i}", [128, D], BF16)) for i in range(2)]
            self.bub = [Buf() for _ in range(2)]
            self.uT = [es.enter_context(sbt(f"uT{i}", [128, NCH, 4, 128], BF16)) for i in range(2)]
            self.buT = [[Buf() for _ in range(4)] for _ in range(2)]
            self.ntp = ntp
            self.TP = [es.enter_context(pst(f"TP{i}", [128, NCH, 128], BF16)) for i in range(ntp)]
            self.bTP = [PBuf() for _ in range(ntp)]
            self.xc = 0
            self.bc = 0

        def load(self, ti):
            b0, nb = tiles[ti]
            self.xslots[ti] = []
            for bi in range(nb):
                xi = self.xc % len(self.xt)
                self.xc += 1
                self.xslots[ti].append(xi)
                S.dma("sp", self.xt[xi][:], xs_d[(b0 + bi) * 128:(b0 + bi + 1) * 128, :],
                      writes=[self.bxt[xi]], dsem=self.xsem[xi])

        def steps(self, ti):
            b0, nb = tiles[ti]
            sl = ti % 2
            ss = self.ss[sl]
            xslots = self.xslots[ti]
            uss = []
            for bi in range(nb):
                uss.append(self.bc % 2)
                self.bc += 1

            def sq(bi):
                xi = xslots[bi]
                bss = self.bss[sl][bi]
                k.act(self.junk[:], self.xt[xi][:], AF.Square, r=[self.bxt[xi]], w=[self.bjunk, bss],
                      accum_out=ss[:, bi:bi + 1])
                k.ts("pool", ss[:, 4 + bi:5 + bi], ss[:, bi:bi + 1], 1.0 / D, EPS, ALU.mult, ALU.add, r=[bss], w=[bss])
                k.tt("pool", ss[:, 8 + bi:9 + bi], ss[:, 4 + bi:5 + bi], nhalf, ALU.pow, r=[bss, bcst], w=[bss])

            def tr(bi):
                xi = xslots[bi]
                us = uss[bi]
                tp = us % self.ntp
                if self.scale_eng == "dve":
                    k.ts("dve", self.ub[us][:], self.xt[xi][:], ss[:, 8 + bi:9 + bi], None, ALU.mult,
                         r=[self.bxt[xi], self.bss[sl][bi]], w=[self.bub[us]])
                else:
                    k.act(self.ub[us][:], self.xt[xi][:], AF.Identity, r=[self.bxt[xi], self.bss[sl][bi]],
                          w=[self.bub[us]], scale=ss[:, 8 + bi:9 + bi])
                for c in range(NCH):
                    k.tr(self.TP[tp][:, c, :], self.ub[us][:, c * 128:(c + 1) * 128], identb,
                         r=[self.bub[us], bcbf], w=[self.bTP[tp]])

            def ev(bi):
                tp = uss[bi] % self.ntp
                k.act(self.uT[sl][:, :, bi, :], self.TP[tp][:, :, :], AF.Identity,
                      r=[self.bTP[tp]], w=[self.buT[sl][bi]])

            if self.ntp >= 2:
                order = [("sq", b) for b in range(nb)] + [("tr", 0)]
                for b in range(1, nb):
                    order += [("tr", b), ("ev", b - 1)]
                order += [("ev", nb - 1)]
            else:
                order = [("sq", b) for b in range(nb)]
                for b in range(nb):
                    order += [("tr", b), ("ev", b)]
            fns = dict(sq=sq, tr=tr, ev=ev)
            return [(lambda f=fns[n], b=b: f(b)) for n, b in order]

        def emit(self, ti):
            self.load(ti)
            for f in self.steps(ti):
                f()

    with ExitStack() as es:
        Wb = es.enter_context(sbt("Wb1a", [128, NCH, 4096], BF16))
        bW = [Buf() for _ in range(8)]
        stage = [es.enter_context(sbt(f"stg{i}", [128, NCH, 512], F32)) for i in range(2)]
        bstage = [Buf() for _ in range(2)]
        stsem = [S.new_sem("stg") for _ in range(2)]
        ut = UT(es, nx=6)
        ut.load(0)
        cw_d, cw_c = convert_weights(NCH, Wb, bW, win_v, 0, 4096, pre_g, stage, bstage, stsem,
                                     order=[2, 3, 4, 5, 0, 1, 6, 7], as_closures=True)
        for f_ in (cw_d[0], cw_d[1], cw_c[0], cw_c[1], cw_d[2], cw_d[3]):
            f_()
        kst = [es.enter_context(sbt(f"kst{i}", [128, H, 512], BF16)) for i in range(2)]
        vst = [es.enter_context(sbt(f"vst{i}", [128, H, 4, 128], BF16)) for i in range(2)]
        qst = [es.enter_context(sbt(f"qst{i}", [128, H, 256], BF16)) for i in range(2)]
        gst = [es.enter_context(sbt(f"gst{i}", [128, H, 256], BF16)) for i in range(2)]
        bkst, bvst, bqst, bgst = ([Buf() for _ in range(2)] for _ in range(4))
        ksem, vsem, qsem, gsem = ([S.new_sem(n) for _ in range(2)] for n in "kvqg")
        gt = [es.enter_context(sbt(f"gt{i}", [128, 256], F32)) for i in range(2)]
        bgt = [Buf() for _ in range(2)]
        PK = [es.enter_context(pst(f"PK{i}", [128, 512], F32)) for i in range(3)]
        PV = [es.enter_context(pst(f"PV{i}", [128, 512], F32)) for i in range(3)]
        bPK = [PBuf() for _ in range(3)]
        bPV = [PBuf() for _ in range(3)]
        pkc = [0]
        pvc = [0]
        gtc = [0]

        def proj_1a(ti, pend):
            def pop():
                if pend:
                    pend.pop(0)()
            b0, nb = tiles[ti]
            sl = ti % 2
            TW = nb * 128
            uT = ut.uT[sl]
            ruT = ut.buT[sl][:nb]
            for h in range(H):
                p = pkc[0] % 3
                pkc[0] += 1
                for c in range(NCH):
                    k.mm(PK[p][:, :TW], Wb[:, c, 1024 + h * 128:1024 + (h + 1) * 128],
                         uT[:, c, :nb, :].rearrange("p b t -> p (b t)") if nb == 4 else uT[:, c, 0, :],
                         c == 0, c == NCH - 1, r=[bW[2], bW[3]] + ruT, w=[bPK[p]])
                k.act(kst[sl][:, h, :TW], PK[p][:, :TW], AF.Identity, r=[bPK[p]], w=[bkst[sl]])
                pop()
            S.dma("pool", KTs.rearrange("h d t -> d h t")[:, :, b0 * 128:b0 * 128 + TW], kst[sl][:, :, :TW],
                  reads=[bkst[sl]], writes=[bKTs], dsem=ksem[sl])
            for bi in range(nb):
                for half in range(2):
                    p = pvc[0] % 3
                    pvc[0] += 1
                    for c in range(NCH):
                        k.mm(PV[p][:, :], uT[:, c, bi, :], Wb[:, c, 2048 + half * 512:2048 + (half + 1) * 512],
                             c == 0, c == NCH - 1, r=[bW[4], bW[5], ruT[bi]], w=[bPV[p]])
                    k.cp("dve", vst[sl][:, half * 4:(half + 1) * 4, bi, :],
                         PV[p][:, :].rearrange("p (h d) -> p h d", h=4), r=[bPV[p]], w=[bvst[sl]])
                    pop()
            S.dma("pool", Vs.rearrange("h p k d -> p h k d")[:, :, b0:b0 + nb, :], vst[sl][:, :, :nb, :],
                  reads=[bvst[sl]], writes=[bVs], dsem=vsem[sl])
            while pend:
                pop()
            mb = my_blocks(b0, nb)
            nq = len(mb)
            if nq == 0:
                return
            j0 = (b0 + mb[0] - 2) // 2
            QW = nq * 128

            def rhs_my(c):
                if nq == 2:
                    return uT[:, c, 0:3:2, :]
                return uT[:, c, mb[0], :]

            def out_my(t):
                if nq == 2:
                    return t.rearrange("p (b t) -> p b t", b=2)
                return t

            for h in range(H):
                p = pkc[0] % 3
                pkc[0] += 1
                for c in range(NCH):
                    k.mm(out_my(PK[p][:, :QW]), Wb[:, c, h * 128:(h + 1) * 128], rhs_my(c),
                         c == 0, c == NCH - 1, r=[bW[0], bW[1]] + ruT, w=[bPK[p]])
                k.ts("dve", qst[sl][:, h, :QW], PK[p][:, :QW], float(1.0 / np.sqrt(128.0)), None, ALU.mult,
                     r=[bPK[p]], w=[bqst[sl]])
            S.dma("pool", Qs.rearrange("h d t -> d h t")[:, :, j0 * 128:j0 * 128 + QW], qst[sl][:, :, :QW],
                  reads=[bqst[sl]], writes=[bQs], dsem=qsem[sl])
            for h in range(H):
                p = pkc[0] % 3
                pkc[0] += 1
                g = gtc[0] % 2
                gtc[0] += 1
                for c in range(NCH):
                    k.mm(out_my(PK[p][:, :QW]), Wb[:, c, 3072 + h * 128:3072 + (h + 1) * 128], rhs_my(c),
                         c == 0, c == NCH - 1, r=[bW[6], bW[7]] + ruT, w=[bPK[p]])
                k.act(gt[g][:, :QW], PK[p][:, :QW], AF.Tanh, r=[bPK[p]], w=[bgt[g]], scale=0.5)
                k.stt("dve", gst[sl][:, h, :QW], gt[g][:, :QW], 1.0, PK[p][:, :QW], ALU.add, ALU.mult,
                      r=[bPK[p], bgt[g]], w=[bgst[sl]])
            S.dma("pool", Gs.rearrange("h d t -> d h t")[:, :, j0 * 128:j0 * 128 + QW], gst[sl][:, :, :QW],
                  reads=[bgst[sl]], writes=[bGs], dsem=gsem[sl])

        for f_ in ut.steps(0):
            f_()
        for ti in range(len(tiles)):
            pend = []
            if ti + 1 < len(tiles):
                ut.load(ti + 1)
                pend = ut.steps(ti + 1)
            if ti == 0:
                u_ = list(pend)
                while len(u_) < 14:
                    u_.append(lambda: None)
                pend = [u_[0], cw_c[2], u_[1], cw_c[3], cw_d[4], cw_d[5], u_[2], u_[3],
                        u_[4], cw_c[4], u_[5], cw_c[5], cw_d[6], cw_d[7], u_[6], u_[7],
                        u_[8], cw_c[6], u_[9], cw_c[7]] + u_[10:]
            proj_1a(ti, pend)
        S.barrier()

    with ExitStack() as es:
        Wb = es.enter_context(sbt("Wb1b", [128, NCH, 2048], BF16))
        bW = [Buf() for _ in range(4)]
        GW = es.enter_context(sbt("GW", [128, 2, NCH, 128], BF16))
        bGW = Buf()
        Dg = es.enter_context(sbt("Dg", [128, NCH, 4, 128], BF16))
        bDg = Buf()
        with ExitStack() as esw:
            stage = [esw.enter_context(sbt(f"stg{i}", [128, NCH, 512], F32)) for i in range(2)]
            bstage = [Buf() for _ in range(2)]
            stsem = [S.new_sem("stg") for _ in range(2)]
            convert_weights(NCH, Wb, bW, win_v, 4096, 2048, pre_g, stage, bstage, stsem)
            S.dma("sp", stage[0][:, :, :].rearrange("p c n -> p (c n)")[:, :2048],
                  gw_d.rearrange("p a c n -> p (a c n)"), writes=[bstage[0]], dsem=stsem[0])
            k.cp("dve", GW[:, :, :, :].rearrange("p a c n -> p (a c n)"),
                 stage[0][:, :, :].rearrange("p c n -> p (c n)")[:, :2048], r=[bstage[0]], w=[bGW])
            for c in range(NCH):
                for j in range(4):
                    k.ts("dve", Dg[:, c, j, :], identb, conv_w(c, j), None, ALU.mult, r=[bcbf, bvecs], w=[bDg])
            S.barrier()
        ut = UT(es, nx=4, ntp=1, scale_eng="act")
        xlb = es.enter_context(sbt("xlb", [128, NCH, 4 + 512], BF16))
        bxlb = [Buf() for _ in range(NCH)]
        for c in range(NCH):
            k.memset("pool", xlb[:, c, 0:4], 0.0, w=[bxlb[c]])
        Hs = es.enter_context(sbt("Hs", [128, NCH, 1 + 512], F32))
        bHs = [Buf() for _ in range(NCH)]
        for c in range(NCH):
            k.memset("pool", Hs[:, c, 0:1], 0.0, w=[bHs[c]])
        xc = es.enter_context(sbt("xc", [128, 2, 512], F32))
        xcb = es.enter_context(sbt("xcb", [128, 2, 512], BF16))
        bxc = [Buf() for _ in range(2)]
        bxcb = [Buf() for _ in range(2)]
        trt = es.enter_context(sbt("trt", [128, 2, 512], F32))
        tit = es.enter_context(sbt("tit", [128, 2, 512], F32))
        tht = es.enter_context(sbt("tht", [128, 2, 512], F32))
        btrt, btit, btht = ([Buf() for _ in range(2)] for _ in range(3))
        At = es.enter_context(sbt("At", [128, NCH, 512], F32))
        Vt = es.enter_context(sbt("Vt", [128, NCH, 512], F32))
        T1 = es.enter_context(sbt("T1", [128, NCH, 512], F32))
        Gsg = es.enter_context(sbt("Gsg", [128, NCH, 256], F32))
        bAt, bVt, bT1, bGsg = ([Buf() for _ in range(NCH)] for _ in range(4))
        tgt = es.enter_context(sbt("tgt", [128, 2, 256], F32))
        btgt = [Buf() for _ in range(2)]
        hsq = es.enter_context(sbt("hsq", [128, 2, 256], BF16))
        bhsq = [Buf() for _ in range(2)]
        lst = [es.enter_context(sbt(f"lst{i}", [128, NCH, 256], BF16)) for i in range(2)]
        blst = [Buf() for _ in range(2)]
        lsem = [S.new_sem("l") for _ in range(2)]
        sst = es.enter_context(sbt("sst", [128, 8], F32))
        bsst = Buf()
        PX = [es.enter_context(pst(f"PX{i}", [128, 512], F32)) for i in range(2)]
        PCs = [es.enter_context(pst(f"PC{i}", [128, 512], F32)) for i in range(2)]
        PR = es.enter_context(pst("PR", [128, 512], F32))
        PI = es.enter_context(pst("PI", [128, 512], F32))
        PG = es.enter_context(pst("PG", [128, 512], F32))
        PS = PG
        bPX = [PBuf() for _ in range(2)]
        bPCs = [PBuf() for _ in range(2)]
        bPR, bPI, bPG = (PBuf() for _ in range(3))
        bPS = bPG
        onesf = es.enter_context(sbt("onesf", [128, 2], F32))
        bonesf = Buf()
        k.memset("pool", onesf[:], 1.0, w=[bonesf])

        def lru_tile(ti, pend):
            def pop():
                if pend:
                    pend.pop(0)()
            b0, nb = tiles[ti]
            sl = ti % 2
            TW = nb * 128
            uT = ut.uT[sl]
            ruT = ut.buT[sl][:nb]
            mb = my_blocks(b0, nb)
            nq = len(mb)
            QW = nq * 128
            j0 = (b0 + mb[0] - 2) // 2 if nq else 0

            def rhs_all(c):
                return uT[:, c, :nb, :].rearrange("p b t -> p (b t)") if nb == 4 else uT[:, c, 0, :]

            def rhs_my(c):
                if nq == 2:
                    return uT[:, c, 0:3:2, :]
                return uT[:, c, mb[0], :]

            def out_my(t):
                if nq == 2:
                    return t.rearrange("p (b t) -> p b t", b=2)
                return t

            def A1(cc):
                p = cc % 2
                PC, bPC = PCs[p], bPCs[p]
                for c in range(NCH):
                    k.mm(PX[p][:, :TW], Wb[:, c, cc * 128:(cc + 1) * 128], rhs_all(c), c == 0, c == NCH - 1,
                         r=[bW[0], bW[1]] + ruT, w=[bPX[p]])
                k.act(xlb[:, cc, 4:4 + TW], PX[p][:, :TW], AF.Identity, r=[bPX[p]], w=[bxlb[cc]])
                for j in range(4):
                    k.mm(PC[:, :TW], Dg[:, cc, j, :], xlb[:, cc, 1 + j:1 + j + TW], j == 0, j == 3,
                         r=[bDg, bxlb[cc]], w=[bPC])
                k.cp("dve", xlb[:, cc, 1:4], xlb[:, cc, TW + 1:TW + 4], r=[bxlb[cc]], w=[bxlb[cc]])
                k.ts("dve", xc[:, p, :TW], PC[:, :TW], conv_b(cc), None, ALU.add, r=[bPC, bvecs], w=[bxc[p]])
                k.ts("dve", xcb[:, p, :TW], PC[:, :TW], conv_b(cc), None, ALU.add, r=[bPC, bvecs], w=[bxcb[p]])

            def A2(cc):
                p = cc % 2
                k.mm(PR[:, :TW], GW[:, 0, cc, :], xcb[:, p, :TW], True, True, r=[bGW, bxcb[p]], w=[bPR])
                k.mm(PI[:, :TW], GW[:, 1, cc, :], xcb[:, p, :TW], True, True, r=[bGW, bxcb[p]], w=[bPI])
                if nq:
                    for c in range(NCH):
                        k.mm(out_my(PG[:, :QW]), Wb[:, c, 1024 + cc * 128:1024 + (cc + 1) * 128], rhs_my(c),
                             c == 0, c == NCH - 1, r=[bW[2], bW[3]] + ruT, w=[bPG])
                k.act(trt[:, p, :TW], PR[:, :TW], AF.Tanh, r=[bPR, bder], w=[btrt[p]], scale=0.5, bias=hba(cc))
                k.act(tit[:, p, :TW], PI[:, :TW], AF.Tanh, r=[bPI, bder], w=[btit[p]], scale=0.5, bias=hbx(cc))
                k.act(At[:, cc, :TW], trt[:, p, :TW], AF.Exp, r=[btrt[p], bder], w=[bAt[cc]], scale=s1(cc), bias=s1(cc))
                k.act(tht[:, p, :TW], trt[:, p, :TW], AF.Tanh, r=[btrt[p], bder], w=[btht[p]], scale=s2(cc), bias=s2(cc))
                k.stt("dve", T1[:, cc, :TW], tit[:, p, :TW], 1.0, xc[:, p, :TW], ALU.add, ALU.mult,
                      r=[btit[p], bxc[p]], w=[bT1[cc]])
                if b0 == 0:
                    k.tt("dve", T1[:, cc, 0:256], T1[:, cc, 0:256], tm[:, :], ALU.mult, r=[bT1[cc], btm], w=[bT1[cc]])
                k.tt("pool", Vt[:, cc, :TW], At[:, cc, :TW], At[:, cc, :TW], ALU.mult, r=[bAt[cc]], w=[bVt[cc]])
                if nq:
                    k.act(tgt[:, p, :QW], PG[:, :QW], AF.Tanh, r=[bPG], w=[btgt[p]], scale=0.5)
                k.stt("dve", Vt[:, cc, :TW], Vt[:, cc, :TW], 1.0, tht[:, p, :TW], ALU.add, ALU.mult,
                      r=[bVt[cc], btht[p]], w=[bVt[cc]])
                if nq:
                    k.stt("dve", Gsg[:, cc, :QW], tgt[:, p, :QW], 1.0, PG[:, :QW], ALU.add, ALU.mult,
                          r=[btgt[p], bPG], w=[bGsg[cc]])

            def hmy(cc):
                if nq == 2:
                    return Hs[:, cc, 1:1 + 512].rearrange("p (b t) -> p b t", b=4)[:, 0:3:2, :]
                return Hs[:, cc, 1 + mb[0] * 128:1 + (mb[0] + 1) * 128]

            A1(0)
            for cc in range(NCH):
                if cc + 1 < NCH:
                    A1(cc + 1)
                A2(cc)
            for cc in range(NCH):
                k.act(Vt[:, cc, :TW], Vt[:, cc, :TW], AF.Sqrt, r=[bVt[cc]], w=[bVt[cc]], scale=0.25)
            for cc in range(NCH):
                k.tt("dve", Vt[:, cc, :TW], T1[:, cc, :TW], Vt[:, cc, :TW], ALU.mult,
                     r=[bT1[cc], bVt[cc]], w=[bVt[cc]])
                k.scan(Hs[:, cc, 1:1 + TW], At[:, cc, :TW], Vt[:, cc, :TW], Hs[:, cc, 0:1],
                       r=[bAt[cc], bVt[cc], bHs[cc]], w=[bHs[cc]])
                pop()
            if nq:
                for cc in range(NCH):
                    p = cc % 2
                    k.act(out_my(hsq[:, p, :QW]), hmy(cc), AF.Square, r=[bHs[cc]], w=[bhsq[p]])
                    for q in range(nq):
                        k.mm(PS[:, q:q + 1], hsq[:, p, q * 128:(q + 1) * 128], onesb[:, 0:1], cc == 0 and q == 0,
                             cc == NCH - 1, r=[bhsq[p], bonesb], w=[bPS])
                    k.stt("dve", out_my(lst[sl][:, cc, :QW]), hmy(cc), lgh(cc), out_my(Gsg[:, cc, :QW]), ALU.mult, ALU.mult,
                          r=[bHs[cc], bder, bGsg[cc]], w=[blst[sl]])
                    pop()
            while pend:
                pop()
            for cc in range(NCH):
                k.act(Hs[:, cc, 0:1], Hs[:, cc, TW:TW + 1], AF.Identity, r=[bHs[cc]], w=[bHs[cc]])
            if nq:
                k.cp("dve", sst[:, 0:nq], PS[:, 0:nq], r=[bPS], w=[bsst])
                k.ts("pool", sst[:, 2:2 + nq], sst[:, 0:nq], 1.0 / D, EPS, ALU.mult, ALU.add, r=[bsst], w=[bsst])
                for q in range(nq):
                    k.tt("pool", rl[:, j0 + q:j0 + q + 1], sst[:, 2 + q:3 + q], nhalf, ALU.pow, r=[bsst, bcst], w=[brl])
                S.dma("pool", Ls.rearrange("c p t -> p c t")[:, :, j0 * 128:j0 * 128 + QW], lst[sl][:, :, :QW],
                      reads=[blst[sl]], writes=[bLs], dsem=lsem[sl])

        ut.emit(0)
        for ti in range(len(tiles)):
            pend = []
            if ti + 1 < len(tiles):
                ut.load(ti + 1)
                pend = ut.steps(ti + 1)
            lru_tile(ti, pend)
        S.barrier()

    with ExitStack() as es_outer:
        attg = es_outer.enter_context(sbt("attg", [128, H, MYT], BF16))
        battg = [[Buf() for _ in range(NG)] for _ in range(H)]

        with ExitStack() as es:
            KT = [es.enter_context(sbt(f"KT{i}", [128, TOK], BF16)) for i in range(2)]
            Vh = [es.enter_context(sbt(f"Vh{i}", [128, NBK, 128], BF16)) for i in range(2)]
            Qh = [es.enter_context(sbt(f"Qh{i}", [128, MYT], BF16)) for i in range(2)]
            Gh = [es.enter_context(sbt(f"Gh{i}", [128, MYT], BF16)) for i in range(2)]
            bKT, bVh, bQh, bGh = ([Buf() for _ in range(2)] for _ in range(4))
            hsem = [[S.new_sem("h") for _ in range(2)] for _ in range(4)]
            NZ = 5
            E = [es.enter_context(sbt(f"E{i}", [128, 512], F32)) for i in range(2)]
            Lb = [es.enter_context(sbt(f"Lb{i}", [128, 512], BF16)) for i in range(3)]
            Wt = [es.enter_context(sbt(f"Wt{i}", [128, 512], BF16)) for i in range(3)]
            SL = [es.enter_context(sbt(f"SL{i}", [128, 512], BF16)) for i in range(4)]
            sqb = [es.enter_context(sbt(f"sqb{i}", [128, 512], BF16)) for i in range(2)]
            bE = [Buf() for _ in E]
            bL = [Buf() for _ in Lb]
            bWt = [Buf() for _ in Wt]
            bSL = [Buf() for _ in SL]
            bsqb = [Buf() for _ in sqb]
            Z = [es.enter_context(pst(f"Z{i}", [128, 512], F32)) for i in range(NZ)]
            O = [es.enter_context(pst(f"O{i}", [128, 512], F32)) for i in range(2)]
            SSB = es.enter_context(pst("SSB", [128, 512], F32))
            bZ = [PBuf() for _ in Z]
            bO = [PBuf() for _ in O]
            bSSB = PBuf()

            def load_head(h):
                s = h % 2
                S.dma("sp", KT[s][:], KTs[h, :, :], reads=[bKTs], writes=[bKT[s]], dsem=hsem[0][s])
                S.dma("sp", Qh[s][:], Qs[h, :, :], reads=[bQs], writes=[bQh[s]], dsem=hsem[2][s])
                S.dma("sp", Vh[s][:], Vs[h, :, :, :], reads=[bVs], writes=[bVh[s]], dsem=hsem[1][s])
                S.dma("sp", Gh[s][:], Gs[h, :, :], reads=[bGs], writes=[bGh[s]], dsem=hsem[3][s])

            its = []
            for h in range(H):
                for G in range(NG):
                    j0 = 4 * G
                    kb_max = 2 * (j0 + 3) + 2
                    for kb in range(kb_max, -1, -1):
                        jj_min = max(j0, (kb - 1) // 2) if kb >= 2 else j0
                        col0 = (jj_min - j0) * 128
                        diag = (kb % 2 == 0 and kb >= 2 and (kb - 2) // 2 >= j0)
                        its.append(dict(h=h, G=G, j0=j0, kb=kb, col0=col0, diag=diag, first=(kb == kb_max),
                                        last=(kb == 0), gi=h * NG + G, sli=kb_max - kb))
            if NG >= 2:
                groups = {}
                for it in its:
                    groups.setdefault(it["gi"], []).append(it)
                glist = [groups[g] for g in sorted(groups)]
                out = []
                for g, A in enumerate(glist):
                    head_len = 6 if g > 0 else 0
                    T = min(6, len(A) - 6) if g + 1 < len(glist) else 0
                    out += A[head_len:len(A) - T]
                    if T:
                        Bh = glist[g + 1][:6]
                        At_ = A[len(A) - T:]
                        for q in range(6):
                            if q < T:
                                out.append(At_[q])
                            out.append(Bh[q])
                assert len(out) == len(its)
                its = out
            n = len(its)

            def s_z(i):
                it = its[i]
                c0 = it["col0"]
                zb = i % NZ
                hs = it["h"] % 2
                q0 = it["j0"] * 128
                k.mm(Z[zb][:, c0:512], KT[hs][:, it["kb"] * 128:(it["kb"] + 1) * 128], Qh[hs][:, q0 + c0:q0 + 512],
                     True, False, r=[bKT[hs], bQh[hs]], w=[bZ[zb]])

            def s_e(i):
                it = its[i]
                c0 = it["col0"]
                zb = i % NZ
                es_ = i % 2
                k.act(E[es_][:, c0:512], Z[zb][:, c0:512], AF.Exp, r=[bZ[zb]], w=[bE[es_]])
                if it["diag"]:
                    k.tt("dve", E[es_][:, c0:c0 + 128], E[es_][:, c0:c0 + 128], cf[:, :], ALU.mult,
                         r=[bE[es_], bcf], w=[bE[es_]])

            def s_l(i):
                it = its[i]
                c0 = it["col0"]
                es_ = i % 2
                ls = i % 3
                if it["kb"] <= 1:
                    k.act(Lb[ls][:, c0:512], E[es_][:, c0:512], AF.Ln, r=[bE[es_], bkm], w=[bL[ls]],
                          bias=1.0, scale=km[:, it["kb"]:it["kb"] + 1])
                else:
                    k.act(Lb[ls][:, c0:512], E[es_][:, c0:512], AF.Ln, r=[bE[es_]], w=[bL[ls]], bias=1.0)

            def s_c(i):
                it = its[i]
                c0 = it["col0"]
                zb = i % NZ
                ls = i % 3
                base = (it["gi"] % 2) * 2
                cur = base + it["sli"] % 2
                nxt = base + (it["sli"] + 1) % 2
                if it["first"]:
                    k.memset("pool", SL[base][:, :], 0.0, w=[bSL[base]])
                    k.memset("pool", SL[base + 1][:, :], 0.0, w=[bSL[base + 1]])
                oc0 = c0 + 128 if it["diag"] else c0
                has_ones = oc0 < 512
                k.mm(Z[zb][:, c0:512], trineg, Lb[ls][:, c0:512], False, not has_ones, r=[bL[ls], bcbf], w=[bZ[zb]])
                if has_ones:
                    k.mm(Z[zb][:, oc0:512], onesneg, SL[cur][:, oc0:512], False, True, r=[bSL[cur], bcbf], w=[bZ[zb]])
                if not it["last"]:
                    k.tt("dve", SL[nxt][:, c0:512], SL[cur][:, c0:512], Lb[ls][:, c0:512], ALU.add,
                         r=[bSL[cur], bL[ls]], w=[bSL[nxt]])

            def s_w(i):
                it = its[i]
                c0 = it["col0"]
                zb = i % NZ
                ws = i % 3
                k.act(Wt[ws][:, c0:512], Z[zb][:, c0:512], AF.Exp, r=[bZ[zb]], w=[bWt[ws]])
                if it["diag"]:
                    k.tt("dve", Wt[ws][:, c0:c0 + 128], Wt[ws][:, c0:c0 + 128], mask01b, ALU.mult,
                         r=[bWt[ws], bcbf], w=[bWt[ws]])

            def s_pv(i):
                it = its[i]
                c0 = it["col0"]
                ws = i % 3
                hs = it["h"] % 2
                ob = it["gi"] % 2
                k.mm(O[ob][:, c0:512], Vh[hs][:, it["kb"], :], Wt[ws][:, c0:512], it["first"], it["last"],
                     r=[bVh[hs], bWt[ws]], w=[bO[ob]])
                if it["last"]:
                    group_end(it)

            def group_end(it):
                h, G, j0 = it["h"], it["G"], it["j0"]
                ob = it["gi"] % 2
                hs = h % 2
                k.act(sqb[ob][:, :], O[ob][:, :], AF.Square, r=[bO[ob]], w=[bsqb[ob]])
                for q in range(4):
                    k.mm(SSB[:, q:q + 1], sqb[ob][:, q * 128:(q + 1) * 128], onesb[:, 0:1], True, True,
                         r=[bsqb[ob], bonesb], w=[bSSB])
                if h == 0:
                    k.cp("dve", ssacc[:, j0:j0 + 4], SSB[:, 0:4], r=[bSSB], w=[bssacc])
                else:
                    k.tt("dve", ssacc[:, j0:j0 + 4], ssacc[:, j0:j0 + 4], SSB[:, 0:4], ALU.add,
                         r=[bSSB, bssacc], w=[bssacc])
                k.stt("dve", attg[:, h, j0 * 128:j0 * 128 + 512], O[ob][:, :], agh(h), Gh[hs][:, j0 * 128:j0 * 128 + 512],
                      ALU.mult, ALU.mult, r=[bO[ob], bder, bGh[hs]], w=[battg[h][G]])

            loaded_heads = {0}
            load_head(0)
            s_z(0)
            s_z(1)
            s_e(0)
            for i in range(n + 1):
                if i < n and its[i]["h"] + 1 < H and (its[i]["h"] + 1) not in loaded_heads and (
                        (NG == 1 and its[i]["sli"] == 2) or (NG >= 2 and its[i]["G"] >= 1 and its[i]["sli"] >= 8)):
                    loaded_heads.add(its[i]["h"] + 1)
                    load_head(its[i]["h"] + 1)
                if i + 1 < n:
                    s_e(i + 1)
                if i + 2 < n:
                    s_z(i + 2)
                if i < n:
                    s_l(i)
                    s_c(i)
                if i >= 1:
                    s_w(i - 1)
                    s_pv(i - 1)
            S.barrier()

        with ExitStack() as es:
            Wo = es.enter_context(sbt("Wo", [128, 16, D], BF16))
            bWo = [Buf() for _ in range(2)]
            stage = [es.enter_context(sbt(f"stg{i}", [128, 16, 512], F32)) for i in range(2)]
            bstage = [Buf() for _ in range(2)]
            stsem = [S.new_sem("stg") for _ in range(2)]
            convert_weights(16, Wo, bWo, wout_v, 0, D, None, stage, bstage, stsem)
            k.ts("pool", ra[:, :], ssacc[:, :], 1.0 / D, EPS, ALU.mult, ALU.add, r=[bssacc], w=[bra])
            for j in range(NQ):
                k.tt("pool", ra[:, j:j + 1], ra[:, j:j + 1], nhalf, ALU.pow, r=[bra, bcst], w=[bra])
            lb = [es.enter_context(sbt(f"lb{i}", [128, NCH, 128], BF16)) for i in range(3)]
            xr = [es.enter_context(sbt(f"xr{i}", [128, D], F32)) for i in range(3)]
            yt = [es.enter_context(sbt(f"yt{i}", [128, D], F32)) for i in range(2)]
            ot = [es.enter_context(sbt(f"ot{i}", [128, D], F32)) for i in range(2)]
            junk = es.enter_context(sbt("junk3", [128, D], BF16))
            ssy = es.enter_context(sbt("ssy", [128, 8], F32))
            blb, bxr, byt, bot = ([Buf() for _ in range(3)] for _ in range(4))
            bjunk = Buf()
            bssy = [Buf() for _ in range(2)]
            lbsem = [S.new_sem("lb") for _ in range(3)]
            xrsem = [S.new_sem("xr") for _ in range(3)]
            osem = [S.new_sem("o") for _ in range(2)]
            YA = [[es.enter_context(pst(f"YA{i}{hh}", [128, 512], F32)) for hh in range(2)] for i in range(2)]
            YL = [[es.enter_context(pst(f"YL{i}{hh}", [128, 512], F32)) for hh in range(2)] for i in range(2)]
            bYA = [[PBuf() for _ in range(2)] for _ in range(2)]
            bYL = [[PBuf() for _ in range(2)] for _ in range(2)]
            all_attg = [b for hb in battg for b in hb]

            def p3_load(j):
                s3 = j % 3
                S.dma("sp", lb[s3][:, :, :], Ls.rearrange("c p t -> p c t")[:, :, j * 128:(j + 1) * 128],
                      reads=[bLs], writes=[blb[s3]], dsem=lbsem[s3])
                S.dma("sp", xr[s3][:], xs_d[(2 * j + 2) * 128:(2 * j + 3) * 128, :], writes=[bxr[s3]], dsem=xrsem[s3])

            p3_load(0)
            if NQ > 1:
                p3_load(1)
            for j in range(NQ):
                s = j % 2
                s3 = j % 3
                if j + 2 < NQ:
                    p3_load(j + 2)
                for hh in range(2):
                    for h in range(H):
                        k.mm(YA[s][hh][:, :], attg[:, h, j * 128:(j + 1) * 128], Wo[:, h, hh * 512:(hh + 1) * 512],
                             h == 0, h == H - 1, r=[battg[h][j // 4], bWo[hh]], w=[bYA[s][hh]])
                for hh in range(2):
                    for c in range(NCH):
                        k.mm(YL[s][hh][:, :], lb[s3][:, c, :], Wo[:, 8 + c, hh * 512:(hh + 1) * 512],
                             c == 0, c == NCH - 1, r=[blb[s3], bWo[hh]], w=[bYL[s][hh]])
                for hh in range(2):
                    k.ts("dve", yt[s][:, hh * 512:(hh + 1) * 512], YA[s][hh][:, :], ra[:, j:j + 1], None, ALU.mult,
                         r=[bYA[s][hh], bra], w=[byt[s]])
                    k.stt("dve", yt[s][:, hh * 512:(hh + 1) * 512], YL[s][hh][:, :], rl[:, j:j + 1],
                          yt[s][:, hh * 512:(hh + 1) * 512], ALU.mult, ALU.add, r=[bYL[s][hh], brl, byt[s]], w=[byt[s]])
                k.act(junk[:], yt[s][:], AF.Square, r=[byt[s]], w=[bjunk, bssy[s]], accum_out=ssy[:, s * 4:s * 4 + 1])
                k.ts("pool", ssy[:, s * 4 + 1:s * 4 + 2], ssy[:, s * 4:s * 4 + 1], 1.0 / D, EPS, ALU.mult, ALU.add,
                     r=[bssy[s]], w=[bssy[s]])
                k.tt("pool", ssy[:, s * 4 + 2:s * 4 + 3], ssy[:, s * 4 + 1:s * 4 + 2], nhalf, ALU.pow,
                     r=[bssy[s], bcst], w=[bssy[s]])
                k.stt("dve", ot[s][:], yt[s][:], ssy[:, s * 4 + 2:s * 4 + 3], pg[:, :], ALU.mult, ALU.mult,
                      r=[byt[s], bssy[s], bpg], w=[bot[s]])
                k.tt("pool", ot[s][:], ot[s][:], xr[s3][:], ALU.add, r=[bot[s], bxr[s3]], w=[bot[s]])
                S.dma("pool", y_d[j * 128:(j + 1) * 128, :], ot[s][:], reads=[bot[s]], dsem=osem[s])
            for s in range(2):
                nc.sync.wait_ge(osem[s].h, osem[s].val)
    return nc, S


def host_inputs(x, meta_tokens, pre_g, post_g, w_in, w_out, att_out_g, lru_out_g, conv_w, conv_b,
                gate_a_w, gate_a_b, gate_x_w, gate_x_b, lru_lambda):
    B, SEQ, _ = x.shape
    NQ = SEQ // 256
    NBK = 2 * NQ + 1
    TOK = NBK * 128
    f32 = np.float32
    bf = ml_dtypes.bfloat16

    def pc(v):
        return np.ascontiguousarray(np.asarray(v, f32).reshape(8, 128).T)

    vecs = np.zeros((128, NV), f32)
    vecs[:, 0:8] = pc(pre_g[0])
    vecs[:, 8:16] = pc(att_out_g[0])
    vecs[:, 16:24] = pc(lru_out_g[0])
    vecs[:, 24:32] = pc(conv_b[0])
    vecs[:, 32:40] = pc(gate_a_b[0])
    vecs[:, 40:48] = pc(gate_x_b[0])
    vecs[:, 48:56] = pc(lru_lambda[0])
    for j in range(4):
        vecs[:, 56 + j:88:4] = pc(conv_w[0, j])
    pg_bc = np.ascontiguousarray(np.broadcast_to(np.asarray(post_g[0], f32)[None, :], (128, D)))
    gatew = np.zeros((128, 2, 8, 128), f32)
    for a, gw in enumerate((gate_a_w[0], gate_x_w[0])):
        gw = np.asarray(gw, f32)
        for nblk in range(16):
            cc, o = divmod(nblk, 2)
            gatew[o * 64:(o + 1) * 64, a, cc, o * 64:(o + 1) * 64] = gw[nblk]
    ar = np.arange(128)
    tri = np.where(ar[:, None] >= ar[None, :], -1.0, 0.0)
    ones = -np.ones((128, 128))
    m01 = np.where(ar[None, :] > ar[:, None], 1.0, 0.0)
    cbf = np.ascontiguousarray(np.stack([tri, ones, m01, np.eye(128)], axis=1).astype(bf))
    cf32 = np.ascontiguousarray(m01.astype(f32))
    w_in0 = np.ascontiguousarray(np.asarray(w_in[0], f32))
    w_out0 = np.ascontiguousarray(np.asarray(w_out[0], f32))
    x = np.asarray(x, f32)
    meta = np.asarray(meta_tokens, f32)
    in_maps = []
    for core in range(2 * B):
        b, par = divmod(core, 2)
        xs = np.zeros((TOK, D), f32)
        valid = np.zeros(TOK, f32)
        if par == 1:
            xs[112:128] = meta
            xs[128:] = x[b, :TOK - 128]
            valid[112:] = 1
        else:
            xs[240:256] = meta
            xs[256:] = x[b, :TOK - 256]
            valid[240:] = 1
        kmask = np.ascontiguousarray(valid[:256].reshape(2, 128).T)
        tmask = np.ascontiguousarray(np.broadcast_to(valid[None, :256], (128, 256)))
        in_maps.append(dict(xs=xs, w_in=w_in0, w_out=w_out0, vecs=vecs, pg_bc=pg_bc, gatew=gatew, cbf=cbf,
                            cf32=cf32, kmask=kmask, tmask=tmask))
    return in_maps


_CACHE = {}


def kernel(**inputs):
    x = np.asarray(inputs["x"])
    B, SEQ, _ = x.shape
    NQ = SEQ // 256
    in_maps = host_inputs(**inputs)
    if SEQ not in _CACHE:
        _CACHE[SEQ] = build(SEQ)[0]
    nc = _CACHE[SEQ]
    res = run_bass_kernel_spmd(nc, in_maps, core_ids=list(range(2 * B)))
    out = np.empty((B, SEQ, D), np.float32)
    for core in range(2 * B):
        b, par = divmod(core, 2)
        y = np.asarray(res.results[core]["y"]).reshape(NQ, 128, D)
        o4 = out[b].reshape(NQ, 2, 128, D)
        o4[:, par] = y
    return out
```

```python
import numpy as np
import ml_dtypes
from contextlib import ExitStack
import concourse.bass as bass
import concourse.mybir as mybir
from concourse.bass_utils import run_bass_kernel_spmd

F32 = mybir.dt.float32
BF16 = mybir.dt.bfloat16
AF = mybir.ActivationFunctionType
ALU = mybir.AluOpType

D = 1024
NCH = 8
H = 8
N_META = 16
EPS = 1e-6
NV = 88


class Sem:
    __slots__ = ("h", "val", "eng")

    def __init__(self, h, eng=None):
        self.h = h
        self.val = 0
        self.eng = eng


class Buf:
    __slots__ = ("name", "w", "r", "excl")

    def __init__(self, name="", excl=False):
        self.name = name
        self.w = {}
        self.r = {}
        self.excl = excl


def PBuf():
    return Buf("psum", True)


class Sched:
    EPOCH = 30000

    def __init__(self, nc):
        self.nc = nc
        self.eng = dict(pe=nc.tensor, act=nc.scalar, dve=nc.vector, pool=nc.gpsimd, sp=nc.sync)
        self.cur = {}
        self.know = {e: {} for e in self.eng}
        self.clock = {}
        self.sems = []
        self.ninst = {e: 0 for e in self.eng}
        self.nwait = {e: 0 for e in self.eng}

    def new_sem(self, name, eng=None):
        s = Sem(self.nc.alloc_semaphore(name=f"{name}_{len(self.sems)}"), eng)
        self.sems.append(s)
        return s

    def _next_tok(self, e):
        s = self.cur.get(e)
        if s is None or s.val >= self.EPOCH:
            s = self.cur[e] = self.new_sem("e" + e, e)
        s.val += 1
        return s, s.val

    def _wait(self, e, sem, val):
        kn = self.know[e]
        if kn.get(sem, 0) >= val:
            return
        self.eng[e].wait_ge(sem.h, val)
        self.nwait[e] += 1
        for s2, v2 in self.clock[(sem, val)].items():
            if kn.get(s2, 0) < v2:
                kn[s2] = v2

    def _deps(self, e, reads, writes):
        for b in reads:
            for sem, val in b.w.items():
                if sem.eng == e and e == "pe":
                    continue
                self._wait(e, sem, val)
            if b.excl:
                for sem, val in b.r.items():
                    if sem.eng == e:
                        continue
                    self._wait(e, sem, val)
        for b in writes:
            for sem, val in b.w.items():
                if sem.eng == e and e == "pe":
                    continue
                self._wait(e, sem, val)
            for sem, val in b.r.items():
                if sem.eng == e and e == "pe":
                    continue
                self._wait(e, sem, val)

    def _record(self, e, sem, val, reads, writes):
        ck = dict(self.know[e])
        ck[sem] = val
        self.clock[(sem, val)] = ck
        for b in reads:
            if b.r.get(sem, 0) < val:
                b.r[sem] = val
        for b in writes:
            b.w = {sem: val}
            b.r = {}

    def op(self, e, inst_fn, reads=(), writes=()):
        self._deps(e, reads, writes)
        inst = inst_fn()
        sem, val = self._next_tok(e)
        inst.then_inc(sem.h, 1)
        self.ninst[e] += 1
        self._record(e, sem, val, reads, writes)

    def dma(self, q, out, in_, reads=(), writes=(), dsem=None):
        self._deps(q, reads, writes)
        inst = self.eng[q].dma_start(out=out, in_=in_)
        dsem.val += 16
        inst.then_inc(dsem.h, 16)
        self.ninst[q] += 1
        self._record(q, dsem, dsem.val, reads, writes)

    def barrier(self):
        for e in self.eng:
            for s in self.sems:
                if s.val > 0 and (s, s.val) in self.clock:
                    self._wait(e, s, s.val)


class K:
    def __init__(self, nc, S):
        self.nc = nc
        self.S = S

    def act(self, out, in_, func, r=(), w=(), **kw):
        self.S.op("act", lambda: self.nc.scalar.activation(out=out, in_=in_, func=func, **kw), r, w)

    def mm(self, out, lhsT, rhs, start, stop, r=(), w=()):
        self.S.op("pe", lambda: self.nc.tensor.matmul(out, lhsT=lhsT, rhs=rhs, start=start, stop=stop,
                                                      skip_group_check=True), r, w)

    def tr(self, out, in_, ident, r=(), w=()):
        self.S.op("pe", lambda: self.nc.tensor.transpose(out=out, in_=in_, identity=ident), r, w)

    def _v(self, e):
        return self.nc.vector if e == "dve" else self.nc.gpsimd

    def tt(self, e, out, in0, in1, op, r=(), w=()):
        self.S.op(e, lambda: self._v(e).tensor_tensor(out=out, in0=in0, in1=in1, op=op), r, w)

    def ts(self, e, out, in0, s1, s2, op0, op1=None, r=(), w=()):
        if op1 is None:
            self.S.op(e, lambda: self._v(e).tensor_scalar(out=out, in0=in0, scalar1=s1, scalar2=None, op0=op0), r, w)
        else:
            self.S.op(e, lambda: self._v(e).tensor_scalar(out=out, in0=in0, scalar1=s1, scalar2=s2, op0=op0, op1=op1), r, w)

    def stt(self, e, out, in0, scalar, in1, op0, op1, r=(), w=()):
        self.S.op(e, lambda: self._v(e).scalar_tensor_tensor(out=out, in0=in0, scalar=scalar, in1=in1,
                                                             op0=op0, op1=op1), r, w)

    def cp(self, e, out, in_, r=(), w=()):
        self.S.op(e, lambda: self._v(e).tensor_copy(out=out, in_=in_), r, w)

    def memset(self, e, ap, val, r=(), w=()):
        self.S.op(e, lambda: self._v(e).memset(ap, val), r, w)

    def scan(self, out, d0, d1, initial, r=(), w=()):
        self.S.op("dve", lambda: self.nc.vector.tensor_tensor_scan(out=out, data0=d0, data1=d1, initial=initial,
                                                                   op0=ALU.mult, op1=ALU.add), r, w)


def build(SEQ):
    NQ = SEQ // 256
    assert NQ % 4 == 0
    NG = NQ // 4
    NBK = 2 * NQ + 1
    TOK = NBK * 128
    MYT = NQ * 128
    tiles = [(b0, min(4, NBK - b0)) for b0 in range(0, NBK, 4)]

    nc = bass.Bass("TRN2", target_bir_lowering=False)
    S = Sched(nc)
    k = K(nc, S)
    _uid = [0]

    def sbt(name, shape, dt):
        _uid[0] += 1
        return nc.sbuf_tensor(f"{name}_{_uid[0]}", shape, dt)

    def pst(name, shape, dt):
        _uid[0] += 1
        return nc.psum_tensor(f"{name}_{_uid[0]}", shape, dt)

    xs_d = nc.dram_tensor("xs", [TOK, D], F32, kind="ExternalInput").ap()
    win_d = nc.dram_tensor("w_in", [D, 6 * D], F32, kind="ExternalInput").ap()
    wout_d = nc.dram_tensor("w_out", [2 * D, D], F32, kind="ExternalInput").ap()
    vecs_d = nc.dram_tensor("vecs", [128, NV], F32, kind="ExternalInput").ap()
    pg_d = nc.dram_tensor("pg_bc", [128, D], F32, kind="ExternalInput").ap()
    gw_d = nc.dram_tensor("gatew", [128, 2, NCH, 128], F32, kind="ExternalInput").ap()
    cbf_d = nc.dram_tensor("cbf", [128, 4, 128], BF16, kind="ExternalInput").ap()
    cf_d = nc.dram_tensor("cf32", [128, 128], F32, kind="ExternalInput").ap()
    km_d = nc.dram_tensor("kmask", [128, 2], F32, kind="ExternalInput").ap()
    tm_d = nc.dram_tensor("tmask", [128, 256], F32, kind="ExternalInput").ap()
    y_d = nc.dram_tensor("y", [MYT, D], F32, kind="ExternalOutput").ap()

    KTs = nc.dram_tensor("KTs", [H, 128, TOK], BF16).ap()
    Vs = nc.dram_tensor("Vs", [H, 128, NBK, 128], BF16).ap()
    Qs = nc.dram_tensor("Qs", [H, 128, MYT], BF16).ap()
    Gs = nc.dram_tensor("Gs", [H, 128, MYT], BF16).ap()
    Ls = nc.dram_tensor("Ls", [NCH, 128, MYT], BF16).ap()
    bKTs, bVs, bQs, bGs, bLs = (Buf(n) for n in ("KTs", "Vs", "Qs", "Gs", "Ls"))

    win_v = win_d.rearrange("(c p) n -> p c n", p=128)
    wout_v = wout_d.rearrange("(c p) n -> p c n", p=128)

    vecs = nc.alloc_sbuf_tensor("vecs_s", [128, NV], F32)
    der = nc.alloc_sbuf_tensor("der_s", [128, 48], F32)
    pg = nc.alloc_sbuf_tensor("pg_s", [128, D], F32)
    cbf = nc.alloc_sbuf_tensor("cbf_s", [128, 4, 128], BF16)
    cf = nc.alloc_sbuf_tensor("cf_s", [128, 128], F32)
    km = nc.alloc_sbuf_tensor("km_s", [128, 2], F32)
    tm = nc.alloc_sbuf_tensor("tm_s", [128, 256], F32)
    cst = nc.alloc_sbuf_tensor("cst_s", [128, 8], F32)
    onesb = nc.alloc_sbuf_tensor("onesb_s", [128, 2], BF16)
    rl = nc.alloc_sbuf_tensor("rl_s", [128, NQ], F32)
    ssacc = nc.alloc_sbuf_tensor("ssacc_s", [128, NQ], F32)
    ra = nc.alloc_sbuf_tensor("ra_s", [128, NQ], F32)
    tmp8 = nc.alloc_sbuf_tensor("tmp8_s", [128, 64], F32)
    bvecs, bder, bpg, bcbf, bcf, bkm, btm, bcst, bonesb, brl, bssacc, bra, btmp8 = (Buf() for _ in range(13))

    trineg = cbf[:, 0, :]
    onesneg = cbf[:, 1, :]
    mask01b = cbf[:, 2, :]
    identb = cbf[:, 3, :]
    pre_g = lambda c: vecs[:, c:c + 1]
    att_g = lambda h: vecs[:, 8 + h:9 + h]
    conv_b = lambda c: vecs[:, 24 + c:25 + c]
    conv_w = lambda c, j: vecs[:, 56 + c * 4 + j:57 + c * 4 + j]
    s1 = lambda c: der[:, c:c + 1]
    s2 = lambda c: der[:, 8 + c:9 + c]
    hba = lambda c: der[:, 16 + c:17 + c]
    hbx = lambda c: der[:, 24 + c:25 + c]
    lgh = lambda c: der[:, 32 + c:33 + c]
    agh = lambda h: der[:, 40 + h:41 + h]
    one_f = cst[:, 0:1]
    nhalf = cst[:, 1:2]

    ld0 = S.new_sem("ld0")
    S.dma("sp", vecs[:], vecs_d[:, :], writes=[bvecs], dsem=ld0)
    S.dma("sp", pg[:], pg_d[:, :], writes=[bpg], dsem=ld0)
    S.dma("sp", cbf[:], cbf_d[:, :, :], writes=[bcbf], dsem=ld0)
    S.dma("sp", cf[:], cf_d[:, :], writes=[bcf], dsem=ld0)
    S.dma("sp", km[:], km_d[:, :], writes=[bkm], dsem=ld0)
    S.dma("sp", tm[:], tm_d[:, :], writes=[btm], dsem=ld0)
    for b_ in (bvecs, bpg, bcbf, bcf, bkm, btm):
        b_.w = {ld0: ld0.val}
    k.memset("pool", cst[:, 0:1], 1.0, w=[bcst])
    k.memset("pool", cst[:, 1:2], -0.5, w=[bcst])
    k.memset("pool", onesb[:], 1.0, w=[bonesb])
    k.act(tmp8[:, 0:8], vecs[:, 48:56], AF.Exp, r=[bvecs], w=[btmp8], scale=-1.0)
    k.act(tmp8[:, 8:16], tmp8[:, 0:8], AF.Ln, r=[btmp8], w=[btmp8], bias=1.0)
    k.ts("dve", der[:, 0:8], tmp8[:, 8:16], -4.0, None, ALU.mult, r=[btmp8], w=[bder])
    k.ts("dve", der[:, 8:16], tmp8[:, 8:16], 4.0, None, ALU.mult, r=[btmp8], w=[bder])
    k.ts("dve", der[:, 16:32], vecs[:, 32:48], 0.5, None, ALU.mult, r=[bvecs], w=[bder])
    k.ts("dve", der[:, 32:40], vecs[:, 16:24], 0.5, None, ALU.mult, r=[bvecs], w=[bder])
    k.ts("dve", der[:, 40:48], vecs[:, 8:16], 0.5, None, ALU.mult, r=[bvecs], w=[bder])

    def my_blocks(b0, nb):
        return [bi for bi in range(nb) if (b0 + bi) % 2 == 0 and (b0 + bi) >= 2]

    def convert_weights(nchunk, Wb, bW, src_v, col0, ncols, scale_fn, stage, bstage, dsems, order=None,
                        as_closures=False):
        ngrp = ncols // 512
        dmas, convs = [], []
        for gi_, g in enumerate(order if order is not None else range(ngrp)):
            dmas.append(lambda gi_=gi_, g=g: _cw_dma(nchunk, src_v, col0, stage, bstage, dsems, gi_, g))
            convs.append(lambda gi_=gi_, g=g: _cw_conv(nchunk, Wb, bW, scale_fn, stage, bstage, gi_, g))
        if as_closures:
            return dmas, convs
        for d_, c_ in zip(dmas, convs):
            d_()
            c_()

    def _cw_dma(nchunk, src_v, col0, stage, bstage, dsems, gi_, g):
        sl = gi_ % 2
        S.dma("sp", stage[sl][:, :nchunk, :], src_v[:, :, col0 + g * 512: col0 + (g + 1) * 512],
              writes=[bstage[sl]], dsem=dsems[sl])

    def _cw_conv(nchunk, Wb, bW, scale_fn, stage, bstage, gi_, g):
        if True:
            sl = gi_ % 2
            for c in range(nchunk):
                o_ap = Wb[:, c, g * 512:(g + 1) * 512]
                if gi_ % 2 == 0:
                    if scale_fn is None:
                        k.cp("dve", o_ap, stage[sl][:, c, :], r=[bstage[sl]], w=[bW[g]])
                    else:
                        k.ts("dve", o_ap, stage[sl][:, c, :], scale_fn(c), None, ALU.mult,
                             r=[bstage[sl], bvecs], w=[bW[g]])
                else:
                    if scale_fn is None:
                        k.act(o_ap, stage[sl][:, c, :], AF.Identity, r=[bstage[sl]], w=[bW[g]])
                    else:
                        k.act(o_ap, stage[sl][:, c, :], AF.Identity, r=[bstage[sl], bvecs], w=[bW[g]],
                              scale=scale_fn(c))

    class UT:
        def __init__(self, es, nx=6, ntp=2, scale_eng="dve"):
            self.scale_eng = scale_eng
            self.xt = [es.enter_context(sbt(f"xt{i}", [128, D], F32)) for i in range(nx)]
            self.bxt = [Buf() for _ in range(nx)]
            self.xsem = [S.new_sem("x") for _ in range(nx)]
            self.junk = es.enter_context(sbt("junk", [128, D], BF16))
            self.bjunk = Buf()
            self.ss = [es.enter_context(sbt(f"ss{i}", [128, 12], F32)) for i in range(2)]
            self.bss = [[Buf() for _ in range(4)] for _ in range(2)]
            self.xslots = {}
            self.ub = [es.enter_context(sbt(f"ub{i}", [128, D], BF16)) for i in range(2)]
            self.bub = [Buf() for _ in range(2)]
            self.uT = [es.enter_context(sbt(f"uT{i}", [128, NCH, 4, 128], BF16)) for i in range(2)]
            self.buT = [[Buf() for _ in range(4)] for _ in range(2)]
            self.ntp = ntp
            self.TP = [es.enter_context(pst(f"TP{i}", [128, NCH, 128], BF16)) for i in range(ntp)]
            self.bTP = [PBuf() for _ in range(ntp)]
            self.xc = 0
            self.bc = 0

        def load(self, ti):
            b0, nb = tiles[ti]
            self.xslots[ti] = []
            for bi in range(nb):
                xi = self.xc % len(self.xt)
                self.xc += 1
                self.xslots[ti].append(xi)
                S.dma("sp", self.xt[xi][:], xs_d[(b0 + bi) * 128:(b0 + bi + 1) * 128, :],
                      writes=[self.bxt[xi]], dsem=self.xsem[xi])

        def steps(self, ti):
            b0, nb = tiles[ti]
            sl = ti % 2
            ss = self.ss[sl]
            xslots = self.xslots[ti]
            uss = []
            for bi in range(nb):
                uss.append(self.bc % 2)
                self.bc += 1

            def sq(bi):
                xi = xslots[bi]
                bss = self.bss[sl][bi]
                k.act(self.junk[:], self.xt[xi][:], AF.Square, r=[self.bxt[xi]], w=[self.bjunk, bss],
                      accum_out=ss[:, bi:bi + 1])
                k.ts("pool", ss[:, 4 + bi:5 + bi], ss[:, bi:bi + 1], 1.0 / D, EPS, ALU.mult, ALU.add, r=[bss], w=[bss])
                k.tt("pool", ss[:, 8 + bi:9 + bi], ss[:, 4 + bi:5 + bi], nhalf, ALU.pow, r=[bss, bcst], w=[bss])

            def tr(bi):
                xi = xslots[bi]
                us = uss[bi]
                tp = us % self.ntp
                if self.scale_eng == "dve":
                    k.ts("dve", self.ub[us][:], self.xt[xi][:], ss[:, 8 + bi:9 + bi], None, ALU.mult,
                         r=[self.bxt[xi], self.bss[sl][bi]], w=[self.bub[us]])
                else:
                    k.act(self.ub[us][:], self.xt[xi][:], AF.Identity, r=[self.bxt[xi], self.bss[sl][bi]],
                          w=[self.bub[us]], scale=ss[:, 8 + bi:9 + bi])
                for c in range(NCH):
                    k.tr(self.TP[tp][:, c, :], self.ub[us][:, c * 128:(c + 1) * 128], identb,
                         r=[self.bub[us], bcbf], w=[self.bTP[tp]])

            def ev(bi):
                tp = uss[bi] % self.ntp
                k.act(self.uT[sl][:, :, bi, :], self.TP[tp][:, :, :], AF.Identity,
                      r=[self.bTP[tp]], w=[self.buT[sl][bi]])

            if self.ntp >= 2:
                order = [("sq", b) for b in range(nb)] + [("tr", 0)]
                for b in range(1, nb):
                    order += [("tr", b), ("ev", b - 1)]
                order += [("ev", nb - 1)]
            else:
                order = [("sq", b) for b in range(nb)]
                for b in range(nb):
                    order += [("tr", b), ("ev", b)]
            fns = dict(sq=sq, tr=tr, ev=ev)
            return [(lambda f=fns[n], b=b: f(b)) for n, b in order]

        def emit(self, ti):
            self.load(ti)
            for f in self.steps(ti):
                f()

    with ExitStack() as es:
        Wb = es.enter_context(sbt("Wb1a", [128, NCH, 4096], BF16))
        bW = [Buf() for _ in range(8)]
        stage = [es.enter_context(sbt(f"stg{i}", [128, NCH, 512], F32)) for i in range(2)]
        bstage = [Buf() for _ in range(2)]
        stsem = [S.new_sem("stg") for _ in range(2)]
        ut = UT(es, nx=6)
        ut.load(0)
        cw_d, cw_c = convert_weights(NCH, Wb, bW, win_v, 0, 4096, pre_g, stage, bstage, stsem,
                                     order=[2, 3, 4, 5, 0, 1, 6, 7], as_closures=True)
        cw_d[0]()
        cw_d[1]()
        for f_ in ut.steps(0):
            f_()
        for f_ in (cw_c[0], cw_c[1], cw_d[2], cw_d[3]):
            f_()
        kst = [es.enter_context(sbt(f"kst{i}", [128, H, 512], BF16)) for i in range(2)]
        vst = [es.enter_context(sbt(f"vst{i}", [128, H, 4, 128], BF16)) for i in range(2)]
        qst = [es.enter_context(sbt(f"qst{i}", [128, H, 256], BF16)) for i in range(2)]
        gst = [es.enter_context(sbt(f"gst{i}", [128, H, 256], BF16)) for i in range(2)]
        bkst, bvst, bqst, bgst = ([Buf() for _ in range(2)] for _ in range(4))
        ksem, vsem, qsem, gsem = ([S.new_sem(n) for _ in range(2)] for n in "kvqg")
        gt = [es.enter_context(sbt(f"gt{i}", [128, 256], F32)) for i in range(2)]
        bgt = [Buf() for _ in range(2)]
        PK = [es.enter_context(pst(f"PK{i}", [128, 512], F32)) for i in range(3)]
        PV = [es.enter_context(pst(f"PV{i}", [128, 512], F32)) for i in range(3)]
        bPK = [PBuf() for _ in range(3)]
        bPV = [PBuf() for _ in range(3)]
        pkc = [0]
        pvc = [0]
        gtc = [0]

        def proj_1a(ti, pend):
            def pop():
                if pend:
                    pend.pop(0)()
            b0, nb = tiles[ti]
            sl = ti % 2
            TW = nb * 128
            uT = ut.uT[sl]
            ruT = ut.buT[sl][:nb]
            for h in range(H):
                p = pkc[0] % 3
                pkc[0] += 1
                for c in range(NCH):
                    k.mm(PK[p][:, :TW], Wb[:, c, 1024 + h * 128:1024 + (h + 1) * 128],
                         uT[:, c, :nb, :].rearrange("p b t -> p (b t)") if nb == 4 else uT[:, c, 0, :],
                         c == 0, c == NCH - 1, r=[bW[2], bW[3]] + ruT, w=[bPK[p]])
                k.act(kst[sl][:, h, :TW], PK[p][:, :TW], AF.Identity, r=[bPK[p]], w=[bkst[sl]])
                pop()
            S.dma("pool", KTs.rearrange("h d t -> d h t")[:, :, b0 * 128:b0 * 128 + TW], kst[sl][:, :, :TW],
                  reads=[bkst[sl]], writes=[bKTs], dsem=ksem[sl])
            for bi in range(nb):
                for half in range(2):
                    p = pvc[0] % 3
                    pvc[0] += 1
                    for c in range(NCH):
                        k.mm(PV[p][:, :], uT[:, c, bi, :], Wb[:, c, 2048 + half * 512:2048 + (half + 1) * 512],
                             c == 0, c == NCH - 1, r=[bW[4], bW[5], ruT[bi]], w=[bPV[p]])
                    k.cp("dve", vst[sl][:, half * 4:(half + 1) * 4, bi, :],
                         PV[p][:, :].rearrange("p (h d) -> p h d", h=4), r=[bPV[p]], w=[bvst[sl]])
                    pop()
            S.dma("pool", Vs.rearrange("h p k d -> p h k d")[:, :, b0:b0 + nb, :], vst[sl][:, :, :nb, :],
                  reads=[bvst[sl]], writes=[bVs], dsem=vsem[sl])
            while pend:
                pop()
            mb = my_blocks(b0, nb)
            nq = len(mb)
            if nq == 0:
                return
            j0 = (b0 + mb[0] - 2) // 2
            QW = nq * 128

            def rhs_my(c):
                if nq == 2:
                    return uT[:, c, 0:3:2, :]
                return uT[:, c, mb[0], :]

            def out_my(t):
                if nq == 2:
                    return t.rearrange("p (b t) -> p b t", b=2)
                return t

            for h in range(H):
                p = pkc[0] % 3
                pkc[0] += 1
                for c in range(NCH):
                    k.mm(out_my(PK[p][:, :QW]), Wb[:, c, h * 128:(h + 1) * 128], rhs_my(c),
                         c == 0, c == NCH - 1, r=[bW[0], bW[1]] + ruT, w=[bPK[p]])
                k.ts("dve", qst[sl][:, h, :QW], PK[p][:, :QW], float(1.0 / np.sqrt(128.0)), None, ALU.mult,
                     r=[bPK[p]], w=[bqst[sl]])
            S.dma("pool", Qs.rearrange("h d t -> d h t")[:, :, j0 * 128:j0 * 128 + QW], qst[sl][:, :, :QW],
                  reads=[bqst[sl]], writes=[bQs], dsem=qsem[sl])
            for h in range(H):
                p = pkc[0] % 3
                pkc[0] += 1
                g = gtc[0] % 2
                gtc[0] += 1
                for c in range(NCH):
                    k.mm(out_my(PK[p][:, :QW]), Wb[:, c, 3072 + h * 128:3072 + (h + 1) * 128], rhs_my(c),
                         c == 0, c == NCH - 1, r=[bW[6], bW[7]] + ruT, w=[bPK[p]])
                k.act(gt[g][:, :QW], PK[p][:, :QW], AF.Tanh, r=[bPK[p]], w=[bgt[g]], scale=0.5)
                k.stt("dve", gst[sl][:, h, :QW], gt[g][:, :QW], 1.0, PK[p][:, :QW], ALU.add, ALU.mult,
                      r=[bPK[p], bgt[g]], w=[bgst[sl]])
            S.dma("pool", Gs.rearrange("h d t -> d h t")[:, :, j0 * 128:j0 * 128 + QW], gst[sl][:, :, :QW],
                  reads=[bgst[sl]], writes=[bGs], dsem=gsem[sl])

        for ti in range(len(tiles)):
            pend = []
            if ti + 1 < len(tiles):
                ut.load(ti + 1)
                pend = ut.steps(ti + 1)
            if ti == 0:
                u_ = list(pend)
                while len(u_) < 14:
                    u_.append(lambda: None)
                pend = [u_[0], cw_c[2], u_[1], cw_c[3], cw_d[4], cw_d[5], u_[2], u_[3],
                        u_[4], cw_c[4], u_[5], cw_c[5], cw_d[6], cw_d[7], u_[6], u_[7],
                        u_[8], cw_c[6], u_[9], cw_c[7]] + u_[10:]
            proj_1a(ti, pend)
        S.barrier()

    with ExitStack() as es:
        Wb = es.enter_context(sbt("Wb1b", [128, NCH, 2048], BF16))
        bW = [Buf() for _ in range(4)]
        GW = es.enter_context(sbt("GW", [128, 2, NCH, 128], BF16))
        bGW = Buf()
        Dg = es.enter_context(sbt("Dg", [128, NCH, 4, 128], BF16))
        bDg = Buf()
        ut = UT(es, nx=4, ntp=1, scale_eng="act")
        ut.load(0)
        for c in range(NCH):
            for j in range(4):
                k.ts("dve", Dg[:, c, j, :], identb, conv_w(c, j), None, ALU.mult, r=[bcbf, bvecs], w=[bDg])
        with ExitStack() as esw:
            stage = [esw.enter_context(sbt(f"stg{i}", [128, NCH, 512], F32)) for i in range(2)]
            bstage = [Buf() for _ in range(2)]
            stsem = [S.new_sem("stg") for _ in range(2)]
            cw_d, cw_c = convert_weights(NCH, Wb, bW, win_v, 4096, 2048, pre_g, stage, bstage, stsem, as_closures=True)
            cw_d[0]()
            cw_d[1]()
            for f_ in ut.steps(0):
                f_()
            cw_c[0]()
            cw_c[1]()
            cw_d[2]()
            cw_d[3]()
            cw_c[2]()
            cw_c[3]()
            S.dma("sp", stage[0][:, :, :].rearrange("p c n -> p (c n)")[:, :2048],
                  gw_d.rearrange("p a c n -> p (a c n)"), writes=[bstage[0]], dsem=stsem[0])
            k.cp("dve", GW[:, :, :, :].rearrange("p a c n -> p (a c n)"),
                 stage[0][:, :, :].rearrange("p c n -> p (c n)")[:, :2048], r=[bstage[0]], w=[bGW])
            S.barrier()
        xlb = es.enter_context(sbt("xlb", [128, NCH, 4 + 512], BF16))
        bxlb = [Buf() for _ in range(NCH)]
        for c in range(NCH):
            k.memset("pool", xlb[:, c, 0:4], 0.0, w=[bxlb[c]])
        Hs = es.enter_context(sbt("Hs", [128, NCH, 1 + 512], F32))
        bHs = [Buf() for _ in range(NCH)]
        for c in range(NCH):
            k.memset("pool", Hs[:, c, 0:1], 0.0, w=[bHs[c]])
        xc = es.enter_context(sbt("xc", [128, 2, 512], F32))
        xcb = es.enter_context(sbt("xcb", [128, 2, 512], BF16))
        bxc = [Buf() for _ in range(2)]
        bxcb = [Buf() for _ in range(2)]
        trt = es.enter_context(sbt("trt", [128, 2, 512], F32))
        tit = es.enter_context(sbt("tit", [128, 2, 512], F32))
        tht = es.enter_context(sbt("tht", [128, 2, 512], F32))
        btrt, btit, btht = ([Buf() for _ in range(2)] for _ in range(3))
        At = es.enter_context(sbt("At", [128, NCH, 512], F32))
        Vt = es.enter_context(sbt("Vt", [128, NCH, 512], F32))
        T1 = es.enter_context(sbt("T1", [128, NCH, 512], F32))
        Gsg = es.enter_context(sbt("Gsg", [128, NCH, 256], F32))
        bAt, bVt, bT1, bGsg = ([Buf() for _ in range(NCH)] for _ in range(4))
        tgt = es.enter_context(sbt("tgt", [128, 2, 256], F32))
        btgt = [Buf() for _ in range(2)]
        hsq = es.enter_context(sbt("hsq", [128, 2, 256], BF16))
        bhsq = [Buf() for _ in range(2)]
        lst = [es.enter_context(sbt(f"lst{i}", [128, NCH, 256], BF16)) for i in range(2)]
        blst = [Buf() for _ in range(2)]
        lsem = [S.new_sem("l") for _ in range(2)]
        sst = es.enter_context(sbt("sst", [128, 8], F32))
        bsst = Buf()
        PX = [es.enter_context(pst(f"PX{i}", [128, 512], F32)) for i in range(2)]
        PCs = [es.enter_context(pst(f"PC{i}", [128, 512], F32)) for i in range(2)]
        PR = es.enter_context(pst("PR", [128, 512], F32))
        PI = es.enter_context(pst("PI", [128, 512], F32))
        PG = es.enter_context(pst("PG", [128, 512], F32))
        PS = PG
        bPX = [PBuf() for _ in range(2)]
        bPCs = [PBuf() for _ in range(2)]
        bPR, bPI, bPG = (PBuf() for _ in range(3))
        bPS = bPG
        onesf = es.enter_context(sbt("onesf", [128, 2], F32))
        bonesf = Buf()
        k.memset("pool", onesf[:], 1.0, w=[bonesf])

        def lru_tile(ti, pend):
            def pop():
                if pend:
                    pend.pop(0)()
            b0, nb = tiles[ti]
            sl = ti % 2
            TW = nb * 128
            uT = ut.uT[sl]
            ruT = ut.buT[sl][:nb]
            mb = my_blocks(b0, nb)
            nq = len(mb)
            QW = nq * 128
            j0 = (b0 + mb[0] - 2) // 2 if nq else 0

            def rhs_all(c):
                return uT[:, c, :nb, :].rearrange("p b t -> p (b t)") if nb == 4 else uT[:, c, 0, :]

            def rhs_my(c):
                if nq == 2:
                    return uT[:, c, 0:3:2, :]
                return uT[:, c, mb[0], :]

            def out_my(t):
                if nq == 2:
                    return t.rearrange("p (b t) -> p b t", b=2)
                return t

            def A1(cc):
                p = cc % 2
                PC, bPC = PCs[p], bPCs[p]
                for c in range(NCH):
                    k.mm(PX[p][:, :TW], Wb[:, c, cc * 128:(cc + 1) * 128], rhs_all(c), c == 0, c == NCH - 1,
                         r=[bW[0], bW[1]] + ruT, w=[bPX[p]])
                k.act(xlb[:, cc, 4:4 + TW], PX[p][:, :TW], AF.Identity, r=[bPX[p]], w=[bxlb[cc]])
                for j in range(4):
                    k.mm(PC[:, :TW], Dg[:, cc, j, :], xlb[:, cc, 1 + j:1 + j + TW], j == 0, j == 3,
                         r=[bDg, bxlb[cc]], w=[bPC])
                k.cp("dve", xlb[:, cc, 1:4], xlb[:, cc, TW + 1:TW + 4], r=[bxlb[cc]], w=[bxlb[cc]])
                k.ts("dve", xc[:, p, :TW], PC[:, :TW], conv_b(cc), None, ALU.add, r=[bPC, bvecs], w=[bxc[p]])
                k.ts("dve", xcb[:, p, :TW], PC[:, :TW], conv_b(cc), None, ALU.add, r=[bPC, bvecs], w=[bxcb[p]])

            def A2(cc):
                p = cc % 2
                k.mm(PR[:, :TW], GW[:, 0, cc, :], xcb[:, p, :TW], True, True, r=[bGW, bxcb[p]], w=[bPR])
                k.mm(PI[:, :TW], GW[:, 1, cc, :], xcb[:, p, :TW], True, True, r=[bGW, bxcb[p]], w=[bPI])
                if nq:
                    for c in range(NCH):
                        k.mm(out_my(PG[:, :QW]), Wb[:, c, 1024 + cc * 128:1024 + (cc + 1) * 128], rhs_my(c),
                             c == 0, c == NCH - 1, r=[bW[2], bW[3]] + ruT, w=[bPG])
                k.act(trt[:, p, :TW], PR[:, :TW], AF.Tanh, r=[bPR, bder], w=[btrt[p]], scale=0.5, bias=hba(cc))
                k.act(tit[:, p, :TW], PI[:, :TW], AF.Tanh, r=[bPI, bder], w=[btit[p]], scale=0.5, bias=hbx(cc))
                k.act(At[:, cc, :TW], trt[:, p, :TW], AF.Exp, r=[btrt[p], bder], w=[bAt[cc]], scale=s1(cc), bias=s1(cc))
                k.act(tht[:, p, :TW], trt[:, p, :TW], AF.Tanh, r=[btrt[p], bder], w=[btht[p]], scale=s2(cc), bias=s2(cc))
                k.stt("dve", T1[:, cc, :TW], tit[:, p, :TW], 1.0, xc[:, p, :TW], ALU.add, ALU.mult,
                      r=[btit[p], bxc[p]], w=[bT1[cc]])
                if b0 == 0:
                    k.tt("dve", T1[:, cc, 0:256], T1[:, cc, 0:256], tm[:, :], ALU.mult, r=[bT1[cc], btm], w=[bT1[cc]])
                k.tt("pool", Vt[:, cc, :TW], At[:, cc, :TW], At[:, cc, :TW], ALU.mult, r=[bAt[cc]], w=[bVt[cc]])
                if nq:
                    k.act(tgt[:, p, :QW], PG[:, :QW], AF.Tanh, r=[bPG], w=[btgt[p]], scale=0.5)
                k.stt("dve", Vt[:, cc, :TW], Vt[:, cc, :TW], 1.0, tht[:, p, :TW], ALU.add, ALU.mult,
                      r=[bVt[cc], btht[p]], w=[bVt[cc]])
                if nq:
                    k.stt("dve", Gsg[:, cc, :QW], tgt[:, p, :QW], 1.0, PG[:, :QW], ALU.add, ALU.mult,
                          r=[btgt[p], bPG], w=[bGsg[cc]])

            def hmy(cc):
                if nq == 2:
                    return Hs[:, cc, 1:1 + 512].rearrange("p (b t) -> p b t", b=4)[:, 0:3:2, :]
                return Hs[:, cc, 1 + mb[0] * 128:1 + (mb[0] + 1) * 128]

            A1(0)
            for cc in range(NCH):
                if cc + 1 < NCH:
                    A1(cc + 1)
                A2(cc)
            for cc in range(NCH):
                k.act(Vt[:, cc, :TW], Vt[:, cc, :TW], AF.Sqrt, r=[bVt[cc]], w=[bVt[cc]], scale=0.25)
            for cc in range(NCH):
                k.tt("dve", Vt[:, cc, :TW], T1[:, cc, :TW], Vt[:, cc, :TW], ALU.mult,
                     r=[bT1[cc], bVt[cc]], w=[bVt[cc]])
                k.scan(Hs[:, cc, 1:1 + TW], At[:, cc, :TW], Vt[:, cc, :TW], Hs[:, cc, 0:1],
                       r=[bAt[cc], bVt[cc], bHs[cc]], w=[bHs[cc]])
                pop()
            if nq:
                for cc in range(NCH):
                    p = cc % 2
                    k.act(out_my(hsq[:, p, :QW]), hmy(cc), AF.Square, r=[bHs[cc]], w=[bhsq[p]])
                    for q in range(nq):
                        k.mm(PS[:, q:q + 1], hsq[:, p, q * 128:(q + 1) * 128], onesb[:, 0:1], cc == 0 and q == 0,
                             cc == NCH - 1, r=[bhsq[p], bonesb], w=[bPS])
                    k.stt("dve", out_my(lst[sl][:, cc, :QW]), hmy(cc), lgh(cc), out_my(Gsg[:, cc, :QW]), ALU.mult, ALU.mult,
                          r=[bHs[cc], bder, bGsg[cc]], w=[blst[sl]])
                    pop()
            while pend:
                pop()
            for cc in range(NCH):
                k.act(Hs[:, cc, 0:1], Hs[:, cc, TW:TW + 1], AF.Identity, r=[bHs[cc]], w=[bHs[cc]])
            if nq:
                k.cp("dve", sst[:, 0:nq], PS[:, 0:nq], r=[bPS], w=[bsst])
                k.ts("pool", sst[:, 2:2 + nq], sst[:, 0:nq], 1.0 / D, EPS, ALU.mult, ALU.add, r=[bsst], w=[bsst])
                for q in range(nq):
                    k.tt("pool", rl[:, j0 + q:j0 + q + 1], sst[:, 2 + q:3 + q], nhalf, ALU.pow, r=[bsst, bcst], w=[brl])
                S.dma("pool", Ls.rearrange("c p t -> p c t")[:, :, j0 * 128:j0 * 128 + QW], lst[sl][:, :, :QW],
                      reads=[blst[sl]], writes=[bLs], dsem=lsem[sl])

        for ti in range(len(tiles)):
            pend = []
            if ti + 1 < len(tiles):
                ut.load(ti + 1)
                pend = ut.steps(ti + 1)
            lru_tile(ti, pend)
        S.barrier()

    with ExitStack() as es_outer:
        attg = es_outer.enter_context(sbt("attg", [128, H, MYT], BF16))
        battg = [[Buf() for _ in range(NG)] for _ in range(H)]

        with ExitStack() as es:
            KT = [es.enter_context(sbt(f"KT{i}", [128, TOK], BF16)) for i in range(2)]
            Vh = [es.enter_context(sbt(f"Vh{i}", [128, NBK, 128], BF16)) for i in range(2)]
            Qh = [es.enter_context(sbt(f"Qh{i}", [128, MYT], BF16)) for i in range(2)]
            Gh = [es.enter_context(sbt(f"Gh{i}", [128, MYT], BF16)) for i in range(2)]
            bKT, bVh, bQh, bGh = ([Buf() for _ in range(2)] for _ in range(4))
            hsem = [[S.new_sem("h") for _ in range(2)] for _ in range(4)]
            NZ = 5
            E = [es.enter_context(sbt(f"E{i}", [128, 512], F32)) for i in range(2)]
            Lb = [es.enter_context(sbt(f"Lb{i}", [128, 512], BF16)) for i in range(3)]
            Wt = [es.enter_context(sbt(f"Wt{i}", [128, 512], BF16)) for i in range(3)]
            SL = [es.enter_context(sbt(f"SL{i}", [128, 512], BF16)) for i in range(4)]
            sqb = [es.enter_context(sbt(f"sqb{i}", [128, 512], BF16)) for i in range(2)]
            bE = [Buf() for _ in E]
            bL = [Buf() for _ in Lb]
            bWt = [Buf() for _ in Wt]
            bSL = [Buf() for _ in SL]
            bsqb = [Buf() for _ in sqb]
            Z = [es.enter_context(pst(f"Z{i}", [128, 512], F32)) for i in range(NZ)]
            O = [es.enter_context(pst(f"O{i}", [128, 512], F32)) for i in range(2)]
            SSB = es.enter_context(pst("SSB", [128, 512], F32))
            bZ = [PBuf() for _ in Z]
            bO = [PBuf() for _ in O]
            bSSB = PBuf()

            def load_head(h):
                s = h % 2
                S.dma("sp", KT[s][:], KTs[h, :, :], reads=[bKTs], writes=[bKT[s]], dsem=hsem[0][s])
                S.dma("sp", Qh[s][:], Qs[h, :, :], reads=[bQs], writes=[bQh[s]], dsem=hsem[2][s])
                S.dma("sp", Vh[s][:], Vs[h, :, :, :], reads=[bVs], writes=[bVh[s]], dsem=hsem[1][s])
                S.dma("sp", Gh[s][:], Gs[h, :, :], reads=[bGs], writes=[bGh[s]], dsem=hsem[3][s])

            its = []
            for h in range(H):
                for G in range(NG):
                    j0 = 4 * G
                    kb_max = 2 * (j0 + 3) + 2
                    for kb in range(kb_max, -1, -1):
                        jj_min = max(j0, (kb - 1) // 2) if kb >= 2 else j0
                        col0 = (jj_min - j0) * 128
                        diag = (kb % 2 == 0 and kb >= 2 and (kb - 2) // 2 >= j0)
                        its.append(dict(h=h, G=G, j0=j0, kb=kb, col0=col0, diag=diag, first=(kb == kb_max),
                                        last=(kb == 0), gi=h * NG + G, sli=kb_max - kb))
            if NG >= 2:
                groups = {}
                for it in its:
                    groups.setdefault(it["gi"], []).append(it)
                glist = [groups[g] for g in sorted(groups)]
                out = []
                for g, A in enumerate(glist):
                    head_len = 6 if g > 0 else 0
                    T = min(6, len(A) - 6) if g + 1 < len(glist) else 0
                    out += A[head_len:len(A) - T]
                    if T:
                        Bh = glist[g + 1][:6]
                        At_ = A[len(A) - T:]
                        for q in range(6):
                            if q < T:
                                out.append(At_[q])
                            out.append(Bh[q])
                assert len(out) == len(its)
                its = out
            n = len(its)

            def s_z(i):
                it = its[i]
                c0 = it["col0"]
                zb = i % NZ
                hs = it["h"] % 2
                q0 = it["j0"] * 128
                k.mm(Z[zb][:, c0:512], KT[hs][:, it["kb"] * 128:(it["kb"] + 1) * 128], Qh[hs][:, q0 + c0:q0 + 512],
                     True, False, r=[bKT[hs], bQh[hs]], w=[bZ[zb]])

            def s_e(i):
                it = its[i]
                c0 = it["col0"]
                zb = i % NZ
                es_ = i % 2
                k.act(E[es_][:, c0:512], Z[zb][:, c0:512], AF.Exp, r=[bZ[zb]], w=[bE[es_]])
                if it["diag"]:
                    k.tt("dve", E[es_][:, c0:c0 + 128], E[es_][:, c0:c0 + 128], cf[:, :], ALU.mult,
                         r=[bE[es_], bcf], w=[bE[es_]])

            def s_l(i):
                it = its[i]
                c0 = it["col0"]
                es_ = i % 2
                ls = i % 3
                if it["kb"] <= 1:
                    k.act(Lb[ls][:, c0:512], E[es_][:, c0:512], AF.Ln, r=[bE[es_], bkm], w=[bL[ls]],
                          bias=1.0, scale=km[:, it["kb"]:it["kb"] + 1])
                else:
                    k.act(Lb[ls][:, c0:512], E[es_][:, c0:512], AF.Ln, r=[bE[es_]], w=[bL[ls]], bias=1.0)

            def s_c(i):
                it = its[i]
                c0 = it["col0"]
                zb = i % NZ
                ls = i % 3
                base = (it["gi"] % 2) * 2
                cur = base + it["sli"] % 2
                nxt = base + (it["sli"] + 1) % 2
                if it["first"]:
                    k.memset("pool", SL[base][:, :], 0.0, w=[bSL[base]])
                    k.memset("pool", SL[base + 1][:, :], 0.0, w=[bSL[base + 1]])
                oc0 = c0 + 128 if it["diag"] else c0
                has_ones = oc0 < 512
                k.mm(Z[zb][:, c0:512], trineg, Lb[ls][:, c0:512], False, not has_ones, r=[bL[ls], bcbf], w=[bZ[zb]])
                if has_ones:
                    k.mm(Z[zb][:, oc0:512], onesneg, SL[cur][:, oc0:512], False, True, r=[bSL[cur], bcbf], w=[bZ[zb]])
                if not it["last"]:
                    k.tt("dve", SL[nxt][:, c0:512], SL[cur][:, c0:512], Lb[ls][:, c0:512], ALU.add,
                         r=[bSL[cur], bL[ls]], w=[bSL[nxt]])

            def s_w(i):
                it = its[i]
                c0 = it["col0"]
                zb = i % NZ
                ws = i % 3
                k.act(Wt[ws][:, c0:512], Z[zb][:, c0:512], AF.Exp, r=[bZ[zb]], w=[bWt[ws]])
                if it["diag"]:
                    k.tt("dve", Wt[ws][:, c0:c0 + 128], Wt[ws][:, c0:c0 + 128], mask01b, ALU.mult,
                         r=[bWt[ws], bcbf], w=[bWt[ws]])

            def s_pv(i):
                it = its[i]
                c0 = it["col0"]
                ws = i % 3
                hs = it["h"] % 2
                ob = it["gi"] % 2
                k.mm(O[ob][:, c0:512], Vh[hs][:, it["kb"], :], Wt[ws][:, c0:512], it["first"], it["last"],
                     r=[bVh[hs], bWt[ws]], w=[bO[ob]])
                if it["last"]:
                    group_end(it)

            def group_end(it):
                h, G, j0 = it["h"], it["G"], it["j0"]
                ob = it["gi"] % 2
                hs = h % 2
                k.act(sqb[ob][:, :], O[ob][:, :], AF.Square, r=[bO[ob]], w=[bsqb[ob]])
                for q in range(4):
                    k.mm(SSB[:, q:q + 1], sqb[ob][:, q * 128:(q + 1) * 128], onesb[:, 0:1], True, True,
                         r=[bsqb[ob], bonesb], w=[bSSB])
                if h == 0:
                    k.cp("dve", ssacc[:, j0:j0 + 4], SSB[:, 0:4], r=[bSSB], w=[bssacc])
                else:
                    k.tt("dve", ssacc[:, j0:j0 + 4], ssacc[:, j0:j0 + 4], SSB[:, 0:4], ALU.add,
                         r=[bSSB, bssacc], w=[bssacc])
                k.stt("dve", attg[:, h, j0 * 128:j0 * 128 + 512], O[ob][:, :], agh(h), Gh[hs][:, j0 * 128:j0 * 128 + 512],
                      ALU.mult, ALU.mult, r=[bO[ob], bder, bGh[hs]], w=[battg[h][G]])

            loaded_heads = {0}
            load_head(0)
            s_z(0)
            s_z(1)
            s_e(0)
            for i in range(n + 1):
                if i < n and its[i]["h"] + 1 < H and (its[i]["h"] + 1) not in loaded_heads and (
                        (NG == 1 and its[i]["sli"] == 2) or (NG >= 2 and its[i]["G"] >= 1 and its[i]["sli"] >= 8)):
                    loaded_heads.add(its[i]["h"] + 1)
                    load_head(its[i]["h"] + 1)
                if i + 1 < n:
                    s_e(i + 1)
                if i + 2 < n:
                    s_z(i + 2)
                if i < n:
                    s_l(i)
                    s_c(i)
                if i >= 1:
                    s_w(i - 1)
                    s_pv(i - 1)
            S.barrier()

        with ExitStack() as es:
            Wo = es.enter_context(sbt("Wo", [128, 16, D], BF16))
            bWo = [Buf() for _ in range(2)]
            stage = [es.enter_context(sbt(f"stg{i}", [128, 16, 512], F32)) for i in range(2)]
            bstage = [Buf() for _ in range(2)]
            stsem = [S.new_sem("stg") for _ in range(2)]
            convert_weights(16, Wo, bWo, wout_v, 0, D, None, stage, bstage, stsem)
            k.ts("pool", ra[:, :], ssacc[:, :], 1.0 / D, EPS, ALU.mult, ALU.add, r=[bssacc], w=[bra])
            for j in range(NQ):
                k.tt("pool", ra[:, j:j + 1], ra[:, j:j + 1], nhalf, ALU.pow, r=[bra, bcst], w=[bra])
            lb = [es.enter_context(sbt(f"lb{i}", [128, NCH, 128], BF16)) for i in range(2)]
            xr = [es.enter_context(sbt(f"xr{i}", [128, D], F32)) for i in range(2)]
            yt = [es.enter_context(sbt(f"yt{i}", [128, D], F32)) for i in range(2)]
            ot = [es.enter_context(sbt(f"ot{i}", [128, D], F32)) for i in range(2)]
            junk = es.enter_context(sbt("junk3", [128, D], BF16))
            ssy = es.enter_context(sbt("ssy", [128, 8], F32))
            blb, bxr, byt, bot = ([Buf() for _ in range(2)] for _ in range(4))
            bjunk = Buf()
            bssy = [Buf() for _ in range(2)]
            lbsem = [S.new_sem("lb") for _ in range(2)]
            xrsem = [S.new_sem("xr") for _ in range(2)]
            osem = [S.new_sem("o") for _ in range(2)]
            YA = [[es.enter_context(pst(f"YA{i}{hh}", [128, 512], F32)) for hh in range(2)] for i in range(2)]
            YL = [[es.enter_context(pst(f"YL{i}{hh}", [128, 512], F32)) for hh in range(2)] for i in range(2)]
            bYA = [[PBuf() for _ in range(2)] for _ in range(2)]
            bYL = [[PBuf() for _ in range(2)] for _ in range(2)]
            all_attg = [b for hb in battg for b in hb]

            def p3_load(j):
                s = j % 2
                S.dma("sp", lb[s][:, :, :], Ls.rearrange("c p t -> p c t")[:, :, j * 128:(j + 1) * 128],
                      reads=[bLs], writes=[blb[s]], dsem=lbsem[s])
                S.dma("sp", xr[s][:], xs_d[(2 * j + 2) * 128:(2 * j + 3) * 128, :], writes=[bxr[s]], dsem=xrsem[s])

            p3_load(0)
            for j in range(NQ):
                s = j % 2
                if j + 1 < NQ:
                    p3_load(j + 1)
                for hh in range(2):
                    for h in range(H):
                        k.mm(YA[s][hh][:, :], attg[:, h, j * 128:(j + 1) * 128], Wo[:, h, hh * 512:(hh + 1) * 512],
                             h == 0, h == H - 1, r=[battg[h][j // 4], bWo[hh]], w=[bYA[s][hh]])
                for hh in range(2):
                    for c in range(NCH):
                        k.mm(YL[s][hh][:, :], lb[s][:, c, :], Wo[:, 8 + c, hh * 512:(hh + 1) * 512],
                             c == 0, c == NCH - 1, r=[blb[s], bWo[hh]], w=[bYL[s][hh]])
                for hh in range(2):
                    k.ts("dve", yt[s][:, hh * 512:(hh + 1) * 512], YA[s][hh][:, :], ra[:, j:j + 1], None, ALU.mult,
                         r=[bYA[s][hh], bra], w=[byt[s]])
                    k.stt("dve", yt[s][:, hh * 512:(hh + 1) * 512], YL[s][hh][:, :], rl[:, j:j + 1],
                          yt[s][:, hh * 512:(hh + 1) * 512], ALU.mult, ALU.add, r=[bYL[s][hh], brl, byt[s]], w=[byt[s]])
                k.act(junk[:], yt[s][:], AF.Square, r=[byt[s]], w=[bjunk, bssy[s]], accum_out=ssy[:, s * 4:s * 4 + 1])
                k.ts("pool", ssy[:, s * 4 + 1:s * 4 + 2], ssy[:, s * 4:s * 4 + 1], 1.0 / D, EPS, ALU.mult, ALU.add,
                     r=[bssy[s]], w=[bssy[s]])
                k.tt("pool", ssy[:, s * 4 + 2:s * 4 + 3], ssy[:, s * 4 + 1:s * 4 + 2], nhalf, ALU.pow,
                     r=[bssy[s], bcst], w=[bssy[s]])
                k.stt("dve", ot[s][:], yt[s][:], ssy[:, s * 4 + 2:s * 4 + 3], pg[:, :], ALU.mult, ALU.mult,
                      r=[byt[s], bssy[s], bpg], w=[bot[s]])
                k.tt("pool", ot[s][:], ot[s][:], xr[s][:], ALU.add, r=[bot[s], bxr[s]], w=[bot[s]])
                S.dma("pool", y_d[j * 128:(j + 1) * 128, :], ot[s][:], reads=[bot[s]], dsem=osem[s])
            for s in range(2):
                nc.sync.wait_ge(osem[s].h, osem[s].val)
    return nc, S


def host_inputs(x, meta_tokens, pre_g, post_g, w_in, w_out, att_out_g, lru_out_g, conv_w, conv_b,
                gate_a_w, gate_a_b, gate_x_w, gate_x_b, lru_lambda):
    B, SEQ, _ = x.shape
    NQ = SEQ // 256
    NBK = 2 * NQ + 1
    TOK = NBK * 128
    f32 = np.float32
    bf = ml_dtypes.bfloat16

    def pc(v):
        return np.ascontiguousarray(np.asarray(v, f32).reshape(8, 128).T)

    vecs = np.zeros((128, NV), f32)
    vecs[:, 0:8] = pc(pre_g[0])
    vecs[:, 8:16] = pc(att_out_g[0])
    vecs[:, 16:24] = pc(lru_out_g[0])
    vecs[:, 24:32] = pc(conv_b[0])
    vecs[:, 32:40] = pc(gate_a_b[0])
    vecs[:, 40:48] = pc(gate_x_b[0])
    vecs[:, 48:56] = pc(lru_lambda[0])
    for j in range(4):
        vecs[:, 56 + j:88:4] = pc(conv_w[0, j])
    pg_bc = np.ascontiguousarray(np.broadcast_to(np.asarray(post_g[0], f32)[None, :], (128, D)))
    gatew = np.zeros((128, 2, 8, 128), f32)
    for a, gw in enumerate((gate_a_w[0], gate_x_w[0])):
        gw = np.asarray(gw, f32)
        for nblk in range(16):
            cc, o = divmod(nblk, 2)
            gatew[o * 64:(o + 1) * 64, a, cc, o * 64:(o + 1) * 64] = gw[nblk]
    ar = np.arange(128)
    tri = np.where(ar[:, None] >= ar[None, :], -1.0, 0.0)
    ones = -np.ones((128, 128))
    m01 = np.where(ar[None, :] > ar[:, None], 1.0, 0.0)
    cbf = np.ascontiguousarray(np.stack([tri, ones, m01, np.eye(128)], axis=1).astype(bf))
    cf32 = np.ascontiguousarray(m01.astype(f32))
    w_in0 = np.ascontiguousarray(np.asarray(w_in[0], f32))
    w_out0 = np.ascontiguousarray(np.asarray(w_out[0], f32))
    x = np.asarray(x, f32)
    meta = np.asarray(meta_tokens, f32)
    in_maps = []
    for core in range(2 * B):
        b, par = divmod(core, 2)
        xs = np.zeros((TOK, D), f32)
        valid = np.zeros(TOK, f32)
        if par == 1:
            xs[112:128] = meta
            xs[128:] = x[b, :TOK - 128]
            valid[112:] = 1
        else:
            xs[240:256] = meta
            xs[256:] = x[b, :TOK - 256]
            valid[240:] = 1
        kmask = np.ascontiguousarray(valid[:256].reshape(2, 128).T)
        tmask = np.ascontiguousarray(np.broadcast_to(valid[None, :256], (128, 256)))
        in_maps.append(dict(xs=xs, w_in=w_in0, w_out=w_out0, vecs=vecs, pg_bc=pg_bc, gatew=gatew, cbf=cbf,
                            cf32=cf32, kmask=kmask, tmask=tmask))
    return in_maps


_CACHE = {}


def kernel(**inputs):
    x = np.asarray(inputs["x"])
    B, SEQ, _ = x.shape
    NQ = SEQ // 256
    in_maps = host_inputs(**inputs)
    if SEQ not in _CACHE:
        _CACHE[SEQ] = build(SEQ)[0]
    nc = _CACHE[SEQ]
    res = run_bass_kernel_spmd(nc, in_maps, core_ids=list(range(2 * B)))
    out = np.empty((B, SEQ, D), np.float32)
    for core in range(2 * B):
        b, par = divmod(core, 2)
        y = np.asarray(res.results[core]["y"]).reshape(NQ, 128, D)
        o4 = out[b].reshape(NQ, 2, 128, D)
        o4[:, par] = y
    return out
```
